# Optimizing a Trainium2 kernel written in Bass

```python
import jax, jax.numpy as jnp
from jax import lax
import numpy as np

D_MODEL = 1024
BATCH = 1
SEQ = 16384
DEPTH = 1
DEC_BATCH = 128
DEC_SEQ = 1
PAST_LEN = 16384
PAGE_SIZE = 128

D_CONV = D_MODEL
CONV_W = 3
N_HEADS = 16
N_KV = 4
GROUP = N_HEADS // N_KV
HEAD_DIM = 64
WINDOW = 128
BLOCK = 128
D_FF = 2816
EPS = 1e-6
N_MOD = 6
ATTN_SCALE = HEAD_DIM ** -0.5
SPLIT_SIZES = (D_CONV, D_CONV, D_CONV, N_HEADS * HEAD_DIM, N_KV * HEAD_DIM, N_KV * HEAD_DIM, D_MODEL, D_MODEL)
IN_COLS = sum(SPLIT_SIZES)
SPLIT_POINTS = tuple(int(v) for v in np.cumsum(SPLIT_SIZES)[:-1])

kernel_name = 'adaln_hybrid_shortconv_swa_sink_convffn_step'


def rmsnorm(x, g):
    xf = x.astype(jnp.float32)
    y = xf * lax.rsqrt(jnp.mean(xf * xf, axis=-1, keepdims=True) + EPS)
    return (y * g.astype(jnp.float32)).astype(x.dtype)


def causal_dwconv(prev, u, w):
    t = u.shape[1]
    full = jnp.concatenate([prev.astype(u.dtype), u], axis=1)
    out = sum(w[i] * full[:, i:i + t] for i in range(CONV_W))
    return out, full[:, -(CONV_W - 1):]


def sink_softmax(s, mask, sink):
    s = jnp.where(mask, s, -jnp.inf)
    m = jnp.maximum(jnp.max(s, axis=-1, keepdims=True), sink)
    p = jnp.exp(s - m)
    return p / (jnp.sum(p, axis=-1, keepdims=True) + jnp.exp(sink - m))


def window_attn_prompt(q, k, v, sinks):
    n, s = q.shape[:2]
    nb = s // BLOCK
    qb = q.reshape(n, nb, BLOCK, N_KV, GROUP, HEAD_DIM)
    kb = k.reshape(n, nb, BLOCK, N_KV, HEAD_DIM)
    vb = v.reshape(n, nb, BLOCK, N_KV, HEAD_DIM)
    pad = ((0, 0), (1, 0), (0, 0), (0, 0), (0, 0))
    kk = jnp.concatenate([jnp.pad(kb, pad)[:, :-1], kb], axis=2)
    vv = jnp.concatenate([jnp.pad(vb, pad)[:, :-1], vb], axis=2)
    d = (jnp.arange(BLOCK) + BLOCK)[:, None] - jnp.arange(2 * BLOCK)[None, :]
    band = (d >= 0) & (d <= WINDOW)
    real = (jnp.arange(nb)[:, None] > 0) | (jnp.arange(2 * BLOCK)[None, :] >= BLOCK)
    mask = band[None] & real[:, None, :]
    sc = jnp.einsum('bnqkgd,bnskd->bnkgqs', qb, kk, preferred_element_type=jnp.float32) * ATTN_SCALE
    sink = sinks.astype(jnp.float32).reshape(N_KV, GROUP)[None, None, :, :, None, None]
    p = sink_softmax(sc, mask[None, :, None, None], sink)
    o = jnp.einsum('bnkgqs,bnskd->bnqkgd', p.astype(vv.dtype), vv)
    return o.reshape(n, s, N_HEADS, HEAD_DIM), k[:, -WINDOW:], v[:, -WINDOW:]


def window_attn_decode(q, k, v, k_prev, v_prev, sinks):
    n, t = q.shape[:2]
    r = k_prev.shape[1]
    kk = jnp.concatenate([k_prev.astype(k.dtype), k], axis=1)
    vv = jnp.concatenate([v_prev.astype(v.dtype), v], axis=1)
    qg = q.reshape(n, t, N_KV, GROUP, HEAD_DIM)
    sc = jnp.einsum('btkgd,bskd->bkgts', qg, kk, preferred_element_type=jnp.float32) * ATTN_SCALE
    d = (r + jnp.arange(t))[:, None] - jnp.arange(r + t)[None, :]
    mask = (d >= 0) & (d <= WINDOW)
    sink = sinks.astype(jnp.float32).reshape(N_KV, GROUP)[None, :, :, None, None]
    p = sink_softmax(sc, mask, sink)
    o = jnp.einsum('bkgts,bskd->btkgd', p.astype(vv.dtype), vv)
    return o.reshape(n, t, N_HEADS, HEAD_DIM), kk[:, -r:], vv[:, -r:]


def decoder_layer(x, c, conv_a_prev, k_prev, v_prev, ffn_prev, w_ada, b_ada, g_mix, w_in, conv_a_w,
                  sinks, w_a_out, w_b_out, w_o, g_ffn, w_up, ffn_conv_w, ffn_conv_b, w_down):
    n, t, _ = x.shape
    ada = jax.nn.silu(c) @ w_ada + b_ada
    sh1, sc1, gt1, sh2, sc2, gt2 = jnp.split(ada[:, None, :], N_MOD, axis=-1)
    h = rmsnorm(x, g_mix) * (1 + sc1) + sh1
    proj = h @ w_in
    xin, b_gate, c_gate, q, k, v, ga, gb = jnp.split(proj, SPLIT_POINTS, axis=-1)
    conv_out, conv_a_new = causal_dwconv(conv_a_prev, c_gate * xin, conv_a_w)
    y_a = (b_gate * conv_out) @ w_a_out
    q = q.reshape(n, t, N_HEADS, HEAD_DIM)
    k = k.reshape(n, t, N_KV, HEAD_DIM)
    v = v.reshape(n, t, N_KV, HEAD_DIM)
    if k_prev is None:
        att, k_new, v_new = window_attn_prompt(q, k, v, sinks)
    else:
        att, k_new, v_new = window_attn_decode(q, k, v, k_prev, v_prev, sinks)
    y_b = att.reshape(n, t, N_HEADS * HEAD_DIM) @ w_b_out
    x = x + gt1 * ((jax.nn.sigmoid(ga) * y_a + jax.nn.sigmoid(gb) * y_b) @ w_o)
    h = rmsnorm(x, g_ffn) * (1 + sc2) + sh2
    up, ffn_new = causal_dwconv(ffn_prev, h @ w_up, ffn_conv_w)
    a_g, a_v = jnp.split(up + ffn_conv_b, 2, axis=-1)
    x = x + gt2 * ((jax.nn.silu(a_g) * a_v) @ w_down)
    return x, conv_a_new, k_new, v_new, ffn_new


def setup_inputs(seed: int = 0) -> dict:
    key = jax.random.key(seed)
    ks = jax.random.split(key, 24)
    f32 = jnp.float32

    def nrm(k, shape, scale):
        return jax.random.normal(k, shape, f32) * scale

    rows = min(WINDOW, PAST_LEN)
    return {
        'x_prompt': nrm(ks[0], (BATCH, SEQ, D_MODEL), 1.0),
        'x_sample': nrm(ks[1], (DEC_BATCH, DEC_SEQ, D_MODEL), 1.0),
        'c_prompt': nrm(ks[2], (BATCH, D_MODEL), 1.0),
        'c_sample': nrm(ks[3], (DEC_BATCH, D_MODEL), 1.0),
        'state_conv_a': nrm(ks[4], (DEPTH, DEC_BATCH, CONV_W - 1, D_CONV), 1.0),
        'cache_k_win': nrm(ks[5], (DEPTH, DEC_BATCH, rows, N_KV, HEAD_DIM), 1.0),
        'cache_v_win': nrm(ks[6], (DEPTH, DEC_BATCH, rows, N_KV, HEAD_DIM), 1.0),
        'state_ffn_conv': nrm(ks[7], (DEPTH, DEC_BATCH, CONV_W - 1, 2 * D_FF), 1.0),
        'w_ada': nrm(ks[8], (DEPTH, D_MODEL, N_MOD * D_MODEL), D_MODEL ** -0.5),
        'b_ada': nrm(ks[9], (DEPTH, N_MOD * D_MODEL), 0.02),
        'g_mix': 1.0 + nrm(ks[10], (DEPTH, D_MODEL), 0.02),
        'w_in': nrm(ks[11], (DEPTH, D_MODEL, IN_COLS), D_MODEL ** -0.5),
        'conv_a_w': nrm(ks[12], (DEPTH, CONV_W, D_CONV), CONV_W ** -0.5),
        'attn_sinks': nrm(ks[13], (DEPTH, N_HEADS), 1.0),
        'w_a_out': nrm(ks[14], (DEPTH, D_CONV, D_MODEL), D_CONV ** -0.5),
        'w_b_out': nrm(ks[15], (DEPTH, N_HEADS * HEAD_DIM, D_MODEL), (N_HEADS * HEAD_DIM) ** -0.5),
        'w_o': nrm(ks[16], (DEPTH, D_MODEL, D_MODEL), D_MODEL ** -0.5),
        'g_ffn': 1.0 + nrm(ks[17], (DEPTH, D_MODEL), 0.02),
        'w_up': nrm(ks[18], (DEPTH, D_MODEL, 2 * D_FF), D_MODEL ** -0.5),
        'ffn_conv_w': nrm(ks[19], (DEPTH, CONV_W, 2 * D_FF), CONV_W ** -0.5),
        'ffn_conv_b': nrm(ks[20], (DEPTH, 2 * D_FF), 0.01),
        'w_down': nrm(ks[21], (DEPTH, D_FF, D_MODEL), D_FF ** -0.5),
        'g_final': 1.0 + nrm(ks[22], (D_MODEL,), 0.02),
    }


def reference(x_prompt, x_sample, c_prompt, c_sample, state_conv_a, cache_k_win, cache_v_win, state_ffn_conv,
              w_ada, b_ada, g_mix, w_in, conv_a_w, attn_sinks, w_a_out, w_b_out, w_o, g_ffn, w_up,
              ffn_conv_w, ffn_conv_b, w_down, g_final):
    xp, xs = x_prompt, x_sample
    nb_p = x_prompt.shape[0]
    ca_p, ca_s, kp, ks_, vp, vs, fp, fs = [], [], [], [], [], [], [], []
    for l in range(DEPTH):
        w = (w_ada[l], b_ada[l], g_mix[l], w_in[l], conv_a_w[l], attn_sinks[l], w_a_out[l], w_b_out[l],
             w_o[l], g_ffn[l], w_up[l], ffn_conv_w[l], ffn_conv_b[l], w_down[l])
        zero_a = jnp.zeros((nb_p, CONV_W - 1, D_CONV), xp.dtype)
        zero_f = jnp.zeros((nb_p, CONV_W - 1, 2 * D_FF), xp.dtype)
        xp, a1, k1, v1, f1 = decoder_layer(xp, c_prompt, zero_a, None, None, zero_f, *w)
        xs, a2, k2, v2, f2 = decoder_layer(xs, c_sample, state_conv_a[l], cache_k_win[l], cache_v_win[l],
                                           state_ffn_conv[l], *w)
        ca_p.append(a1); ca_s.append(a2); kp.append(k1); ks_.append(k2)
        vp.append(v1); vs.append(v2); fp.append(f1); fs.append(f2)
    y_prompt = rmsnorm(xp, g_final)
    y_sample = rmsnorm(xs, g_final)
    return (y_prompt, y_sample, jnp.stack(ca_p), jnp.stack(ca_s), jnp.stack(kp), jnp.stack(ks_),
            jnp.stack(vp), jnp.stack(vs), jnp.stack(fp), jnp.stack(fs))
```

```python
import numpy as np
import concourse.bass as bass
import concourse.mybir as mybir
from concourse.bass_utils import run_bass_kernel_spmd

F32 = mybir.dt.float32
BF16 = mybir.dt.bfloat16
AF = mybir.ActivationFunctionType
ALU = mybir.AluOpType

ENGS = ("pe", "act", "dve", "pool", "sp")
NCORES = 8
D = 1024
NCOL = 1328
C_SB = 254
C_SMP = 286
C_OWN = 304
MC0 = 252
MW = NCOL - MC0
NSLOT = 4
SEGB = [0, 128, 256, 304] + [304 + 128 * i for i in range(1, 9)]
PF_GM, PF_CW, PF_GF, PF_FW, PF_FB, PF_BA, PF_FL, NPF = 0, 8, 32, 40, 172, 216, 264, 265
EPS = 1e-6


def segs(c0, c1):
    return [i for i in range(len(SEGB) - 1) if SEGB[i] < c1 and SEGB[i + 1] > c0]


class Op:
    __slots__ = ("eng", "emit", "idx", "deps", "flag", "ticket", "dma_sem", "dma_cnt", "waits")

    def __init__(self, eng, emit):
        self.eng = eng
        self.emit = emit
        self.deps = []
        self.flag = False
        self.ticket = 0
        self.dma_sem = None
        self.dma_cnt = 0
        self.waits = []


class Prog:
    def __init__(self, nc):
        self.nc = nc
        self.streams = {e: [] for e in ENGS}
        self.last_writer = {}
        self.readers = {}
        self.dma_counts = {}
        self.final_dma = []
        self.pending_dma = []
        self.bar_deps = {}
        self.open_group = {}
        self.last_fin = None
        self.alias = {}
        self._bank = 0

    def bank(self):
        b = self._bank
        self._bank = (b + 1) % 8
        return b

    def add_alias(self, name, w0, w1):
        self.alias[name] = True

    def phase_switch(self, emit):
        return self._add(Op("dve", emit), (), ("scrtok",))

    def _add(self, op, reads, writes):
        if any((n in self.alias) for n in reads) or any((n in self.alias) for n in writes):
            reads = tuple(reads) + ("scrtok",)
        deps = set()
        for r in reads:
            lw = self.last_writer.get(r)
            if lw is not None:
                deps.add(lw)
        for w in writes:
            lw = self.last_writer.get(w)
            if lw is not None:
                deps.add(lw)
            for rd in self.readers.get(w, ()):
                deps.add(rd)
        bd = self.bar_deps.pop(op.eng, None)
        if bd:
            deps.update(bd)
        deps.discard(op)
        op.deps = list(deps)
        for r in reads:
            self.readers.setdefault(r, []).append(op)
        for w in writes:
            self.last_writer[w] = op
            self.readers[w] = []
        op.idx = len(self.streams[op.eng])
        self.streams[op.eng].append(op)
        return op

    def op(self, eng, emit, reads=(), writes=()):
        return self._add(Op(eng, emit), tuple(reads), tuple(writes))

    def dma(self, eng, sem_key, emit, reads=(), writes=(), final=False):
        op = Op(eng, emit)
        c = self.dma_counts.get(sem_key, 0) + 1
        self.dma_counts[sem_key] = c
        op.dma_sem = sem_key
        op.dma_cnt = 16 * c
        self._add(op, tuple(reads), tuple(writes))
        if sem_key == "fin":
            prev = self.last_fin
            if prev is not None and prev not in op.deps:
                op.deps.append(prev)
            self.last_fin = op
        self.open_group.setdefault(sem_key, []).append(op)
        self.pending_dma.append(op)
        if final:
            self.final_dma.append(op)
        return op

    def end_group(self, key):
        tot = 16 * self.dma_counts.get(key, 0)
        for op in self.open_group.get(key, []):
            op.dma_cnt = tot
        self.open_group[key] = []

    def barrier(self):
        lasts = [self.streams[e][-1] for e in ENGS if self.streams[e]]
        lasts += self.pending_dma
        self.pending_dma = []
        self.bar_deps = {e: list(lasts) for e in ENGS}

    def finalize_and_emit(self):
        nc = self.nc
        for e in ENGS:
            for op in self.streams[e]:
                for d in op.deps:
                    if d.dma_sem is None:
                        d.flag = True
        for e in ENGS:
            t = 0
            for op in self.streams[e]:
                if op.dma_sem is None and op.flag:
                    t += 1
                    op.ticket = t
        sem_names = ["eng_" + e for e in ENGS] + ["dma_" + str(k) for k in self.dma_counts]
        sems = {n: nc.alloc_semaphore(name=n) for n in sem_names}
        nwaits = 0
        for e in ENGS:
            waited = {}
            for op in self.streams[e]:
                need = {}
                for d in op.deps:
                    if d.dma_sem is not None:
                        k, v = "dma_" + str(d.dma_sem), d.dma_cnt
                    else:
                        k, v = "eng_" + d.eng, d.ticket
                    if need.get(k, 0) < v:
                        need[k] = v
                op.waits = []
                for k, v in need.items():
                    if waited.get(k, 0) < v:
                        waited[k] = v
                        op.waits.append((k, v))
                        nwaits += 1
        final_waits = {}
        for op in self.final_dma:
            k = "dma_" + str(op.dma_sem)
            final_waits[k] = max(final_waits.get(k, 0), op.dma_cnt)
        engmap = {"pe": "tensor", "act": "scalar", "dve": "vector", "pool": "gpsimd", "sp": "sync"}
        with nc.Block() as block:
            for e in ENGS:
                ops = self.streams[e]

                def body(engine, ops=ops, e=e):
                    for op in ops:
                        for k, v in op.waits:
                            engine.wait_ge(sems[k], v)
                        ins = op.emit(engine)
                        if op.dma_sem is not None:
                            ins.then_inc(sems["dma_" + str(op.dma_sem)], 16)
                        elif op.flag:
                            ins.then_inc(sems["eng_" + e], 1)
                    if e == "sp":
                        for k, v in final_waits.items():
                            engine.wait_ge(sems[k], v)

                getattr(block, engmap[e])(body)
        stats = {e: len(self.streams[e]) for e in ENGS}
        stats["sems"] = len(sem_names)
        stats["waits"] = nwaits
        return stats


class Builder:
    def __init__(self, debug=False, halves=None):
        import os
        self.debug = debug
        if halves is None:
            halves = tuple(int(x) for x in os.environ.get("K_HALVES", "0,1").split(","))
        self.halves = halves
        nc = self.nc = bass.Bass("TRN2", target_bir_lowering=False)
        self.P = Prog(nc)
        self.taps = []
        self._decl()

    def din(self, name, shape):
        return self.nc.dram_tensor(name, list(shape), F32, kind="ExternalInput").ap()

    def dout(self, name, shape):
        return self.nc.dram_tensor(name, list(shape), F32, kind="ExternalOutput").ap()

    def sb(self, name, shape, dt):
        return self.nc.alloc_sbuf_tensor(name, list(shape), dt)

    def _decl(self):
        nc = self.nc
        self.x_d = self.din("x", [2352, D])
        self.cT_d = self.din("cT", [128, 8 * 48])
        self.pf_d = self.din("pf", [128, NPF])
        self.pr_d = self.din("pr", [1, 1040])
        self.cst_d = self.din("cst", [128, 384])
        self.WS_d = self.din("WS", [160, 128, 8, 128])
        self.wv_d = self.din("wv", [D, 256])
        self.wk_d = self.din("wk", [D, 256])
        self.wo_d = self.din("wo", [D, D])
        self.wd_d = self.din("wd", [2816, D])
        self.ck_d = self.din("ck", [16, 128, 256])
        self.cv_d = self.din("cv", [16, 128, 256])
        self.sca_d = self.din("sca", [16, 2, D])
        self.sff_d = self.din("sff", [16, 2, 5632])
        self.y_d = self.dout("y", [2048, D])
        self.ys_d = self.dout("ys", [16, D])
        self.ncap_d = self.dout("ncap", [2, D])
        self.ncas_d = self.dout("ncas", [16, 2, D])
        self.nkp_d = self.dout("nkp", [128, 256])
        self.nks_d = self.dout("nks", [16, 128, 256])
        self.nvp_d = self.dout("nvp", [128, 256])
        self.nvs_d = self.dout("nvs", [16, 128, 256])
        self.nfp_d = self.dout("nfp", [2, 5632])
        self.nfs_d = self.dout("nfs", [16, 2, 5632])
        self.x1_d = nc.dram_tensor("x1scr", [2048, D], F32, kind="Internal").ap()
        self.vs_d = nc.dram_tensor("vsscr", [16, 256], F32, kind="Internal").ap()

        NE = 8 * NCOL + 2 * NCOL + 10 * 256 + 8 * 1024
        self.EREG = self.sb("EREG", [128, NE], BF16)
        o = 0
        self.HT = self.EREG[:, o:o + 8 * NCOL].rearrange("p (k n) -> p k n", k=8); o += 8 * NCOL
        self.KT = self.EREG[:, o:o + 2 * NCOL].rearrange("p (k n) -> p k n", k=2); o += 2 * NCOL
        self.VT = self.EREG[:, o:o + 2560].rearrange("p (k n) -> p k n", k=10); o += 2560
        self.WO = self.EREG[:, o:o + 8192].rearrange("p (k n) -> p k n", k=8); o += 8192
        self.WD = self.EREG[:, 0:22 * 1024].rearrange("p (k n) -> p k n", k=22)
        self.MREGF = self.sb("MREG", [128, 24 * MW], BF16)
        self.MREG = self.MREGF[:, :].rearrange("p (k n) -> p k n", k=24)
        self.WV = self.MREGF[:, 22 * MW:22 * MW + 2048].rearrange("p (k n) -> p k n", k=8)
        self.KCT = self.MREGF[:, 16 * MW:16 * MW + 4096].rearrange("p (c b s) -> p c b s", c=2, b=16)
        self.VC = self.MREGF[:, 16 * MW + 4096:16 * MW + 8192].rearrange("p (b f) -> p b f", b=16)
        self.WR = self.sb("WR", [128, NSLOT, 4, 8, 128], BF16)
        self.ADAT = self.sb("ADAT", [128, 48, 48], F32)
        self.GT1BC = self.sb("GT1BC", [128, D], F32)
        self.GT2BC = self.sb("GT2BC", [128, D], F32)
        self.GT1SB = self.sb("GT1SB", [48, D], F32)
        self.GT2SB = self.sb("GT2SB", [48, D], F32)
        self.GFIN = self.sb("GFIN", [128, D], F32)
        self.XR = self.sb("XR", [128, 2, D], F32)
        self.XN = self.sb("XN", [128, 2, D], BF16)
        self.JK = self.sb("JK", [128, D], BF16)
        self.CST = self.sb("CST", [128, 384], F32)
        self.IDB = self.sb("IDB", [128, 128], BF16)
        self.MPREV = self.sb("MPREV", [128, 4, 128], BF16)
        self.MCUR = self.sb("MCUR", [128, 4, 128], BF16)
        self.MPREV0 = self.sb("MPREV0", [128, 4, 128], BF16)
        self.SINKP = self.sb("SINKP", [128, 2, 4], F32)
        self.ONESB = self.sb("ONESB", [128, 128], BF16)
        self.PF = self.sb("PF", [128, NPF], F32)
        self.SINKB = self.sb("SINKB", [128, 16], F32)
        self.SINKE = self.sb("SINKE", [128, 16], F32)
        self.EPSC = self.sb("EPSC", [128, 1], F32)
        self.TOK = self.sb("TOK", [128, 1], F32)
        self.KCAR = self.sb("KCAR", [128, 2, 128], BF16)
        self.VCAR = self.sb("VCAR", [128, 256], BF16)
        self.SS = self.sb("SS", [128, 64], F32)
        self.RS = self.sb("RS", [128, 64], F32)
        self.ULAST = self.sb("ULAST", [128, 8, 2], F32)
        self.UPLAST = self.sb("UPLAST", [128, 44, 2], F32)
        self.SCR = self.sb("SCR", [128, 6784], F32)
        self.CT = self.SCR[:, 0:384].rearrange("p (k t) -> p k t", k=8)
        self.SCT = self.sb("SCT", [128, 8, 48], BF16)
        self.USM = self.SCR[:, 5268:5668].rearrange("p (k t) -> p k t", k=8)
        self.BSM = self.SCR[:, 5668:6052].rearrange("p (k t) -> p k t", k=8)
        self.CSM = self.SCR[:, 6052:6436].rearrange("p (k t) -> p k t", k=8)
        self.STA = self.SCR[:, 6436:6692].rearrange("p (k t) -> p k t", k=8)
        self.PS = [nc.alloc_psum_tensor("ps%d" % i, [128, 512], F32) for i in range(8)]
        self.ss_col = 0

    def psb(self, b):
        return self.PS[b][:, :].bitcast(BF16)

    def act(self, out, in_, func, reads, writes, bias=None, scale=None, accum_out=None):
        kw = {}
        if bias is not None:
            kw["bias"] = bias
        if scale is not None:
            kw["scale"] = scale
        if accum_out is not None:
            kw["accum_out"] = accum_out
        return self.P.op("act", lambda e: e.activation(out=out, in_=in_, func=func, **kw), reads, writes)

    def tt(self, eng, out, in0, in1, op, reads, writes):
        return self.P.op(eng, lambda e: e.tensor_tensor(out=out, in0=in0, in1=in1, op=op), reads, writes)

    def ts(self, eng, out, in0, s1, s2, op0, op1, reads, writes):
        if op1 is None:
            return self.P.op(eng, lambda e: e.tensor_scalar(out=out, in0=in0, scalar1=s1, scalar2=None, op0=op0), reads, writes)
        return self.P.op(eng, lambda e: e.tensor_scalar(out=out, in0=in0, scalar1=s1, scalar2=s2, op0=op0, op1=op1), reads, writes)

    def stt(self, eng, out, in0, scalar, in1, op0, op1, reads, writes):
        return self.P.op(eng, lambda e: e.scalar_tensor_tensor(out=out, in0=in0, scalar=scalar, in1=in1, op0=op0, op1=op1), reads, writes)

    def copy(self, eng, out, in_, reads, writes):
        if eng == "act":
            return self.act(out, in_, AF.Copy, reads, writes)
        return self.P.op(eng, lambda e: e.tensor_copy(out=out, in_=in_), reads, writes)

    def mm(self, seq, reads, writes, start=True, stop=True):
        n = len(seq)

        def emit(e):
            ins = None
            for i, (o, l, r) in enumerate(seq):
                ins = e.matmul(o, lhsT=l, rhs=r, start=(start and i == 0), stop=(stop and i == n - 1))
            return ins
        return self.P.op("pe", emit, reads, writes)

    def mm_multi(self, groups, reads, writes):
        def emit(e):
            ins = None
            for seq in groups:
                n = len(seq)
                for i, (o, l, r) in enumerate(seq):
                    ins = e.matmul(o, lhsT=l, rhs=r, start=(i == 0), stop=(i == n - 1))
            return ins
        return self.P.op("pe", emit, reads, writes)

    def tr(self, items, reads, writes):
        def emit(e):
            ins = None
            for (o, i_, idn) in items:
                ins = e.transpose(out=o, in_=i_, identity=idn)
            return ins
        return self.P.op("pe", emit, reads, writes)

    def ld(self, key, out, in_, reads=(), writes=(), eng="sp"):
        return self.P.dma(eng, key, lambda e: e.dma_start(out=out, in_=in_), reads, writes)

    def st(self, key, out, in_, reads=(), writes=(), final=True, eng="sp"):
        return self.P.dma(eng, key, lambda e: e.dma_start(out=out, in_=in_), reads, writes, final=final)

    def tap(self, name, ap, shape, reads):
        if not self.debug:
            return
        d = self.nc.dram_tensor("dbg_" + name, list(shape), ap.dtype, kind="ExternalOutput").ap()
        self.taps.append(name)
        self.st("dbg_" + name, d, ap, reads=reads)

    def w_init(self):
        seq = list(range(0, 16)) + list(range(48, 58)) + list(range(16, 48)) + list(range(58, 158))
        if 1 in self.halves:
            seq += list(range(48, 158))
        self.wseq = seq
        groups = []
        self.wpos = []
        for pos, d in enumerate(seq):
            if groups and groups[-1][1] < 4 and groups[-1][0] + groups[-1][1] == d:
                g = groups[-1]
                groups[-1] = (g[0], g[1] + 1, g[2])
            else:
                groups.append((d, 1, pos))
            self.wpos.append((len(groups) - 1, pos - groups[-1][2]))
        self.wgroups = groups
        self.w_next = 0

    def wp(self, h, off):
        if h == 0:
            return 16 + off if off < 10 else 48 + off
        return 158 + off

    def w_use(self, pos):
        g, sub = self.wpos[pos]
        while self.w_next < len(self.wgroups) and self.w_next < g + NSLOT - 1:
            gl = self.w_next
            ds, n, _ = self.wgroups[gl]
            slot = gl % NSLOT
            self.P.dma("pool", "w%d" % slot,
                       lambda e, slot=slot, ds=ds, n=n: e.dma_start(
                           out=self.WR[:, slot, 0:n], in_=self.WS_d[ds:ds + n].rearrange("c p k n -> p c k n")),
                       writes=[("w", slot)])
            self.w_next += 1
        return self.WR[:, g % NSLOT, sub], ("w", g % NSLOT)

    def p0_consts(self):
        P = self.P
        self.ld("p0", self.CST[:], self.cst_d[:, :], writes=["CST"])
        self.ld("p0", self.PF[:], self.pf_d[:, :], writes=["PF"])
        self.ld("p0", self.SCR[:, 0:384], self.cT_d[:, :], writes=["CT"])
        self.ld("p0", self.GFIN[:], self.pr_d[0:1, 0:1024].broadcast_to([128, 1024]), writes=["GFIN"])
        self.ld("p0", self.SINKB[:], self.pr_d[0:1, 1024:1040].broadcast_to([128, 16]), writes=["SINKB"])
        self.P.end_group("p0")
        self.copy("dve", self.IDB[:], self.CST[:, 0:128], ["CST"], ["IDB"])
        mp = self.CST[:, 128:256].unsqueeze(1).broadcast_to([128, 4, 128])
        mc = self.CST[:, 256:384].unsqueeze(1).broadcast_to([128, 4, 128])
        self.ts("dve", self.MPREV[:], mp, -1.0, 30000.0, ALU.add, ALU.mult, ["CST"], ["masks"])
        self.ts("dve", self.MCUR[:], mc, -1.0, 30000.0, ALU.add, ALU.mult, ["CST"], ["masks"])
        TMPM = self.SCR[:, 1024:1536].rearrange("p (g q) -> p g q", g=4)
        self.ts("dve", TMPM, mp, self.PF[:, PF_FL:PF_FL + 1], -1.0, ALU.mult, ALU.add, ["CST", "PF"], ["tmpm"])
        self.ts("dve", self.MPREV0[:], TMPM, 30000.0, None, ALU.mult, None, ["tmpm"], ["masks"])
        P.op("dve", lambda e: e.memset(self.ONESB[:], 1.0), [], ["ONESB"])
        P.op("dve", lambda e: e.memset(self.EPSC[:], EPS), [], ["EPSC"])
        self.act(self.SINKE[:], self.SINKB[:], AF.Exp, ["SINKB"], ["SINKE"])
        for c in range(2):
            for par in range(2):
                kv = 2 * c + par
                self.copy("dve", self.SINKP[par * 64:par * 64 + 64, c, :], self.SINKE[par * 64:par * 64 + 64, kv * 4:kv * 4 + 4],
                          ["SINKE"], ["SINKP"])
        self.act(self.SCT[:], self.CT, AF.Silu, ["CT"], ["SCT"])

    def cache_copies(self):
        self.st("fin", self.nks_d[:, 0:127, :], self.ck_d[:, 1:128, :])
        self.st("fin", self.nvs_d[:, 0:127, :], self.cv_d[:, 1:128, :])
        self.st("fin", self.ncas_d[:, 0, :], self.sca_d[:, 1, :])
        self.st("fin", self.nfs_d[:, 0, :], self.sff_d[:, 1, :])

    def p1_ada(self, part, ms=None, finish=True, bank=None):
        P = self.P
        if ms is None:
            ms = range(0, 16) if part == 0 else range(16, 48)
        for m in ms:
            w, wr = self.w_use(m if m < 16 else 26 + (m - 16))
            b = P.bank() if bank is None else bank
            self.mm([(self.PS[b][:, 0:48], w[:, k, :], self.SCT[:, k, :]) for k in range(8)],
                    [wr, "SCT"], [("ps", b)])
            self.act(self.ADAT[:, m, :], self.PS[b][:, 0:48], AF.Identity, [("ps", b), "PF"], [("ada", m)],
                     bias=self.PF[:, PF_BA + m:PF_BA + m + 1])
        if not finish:
            return
        if part == 0:
            gm = self.PF[:, PF_GM:PF_GM + 8].unsqueeze(2).broadcast_to([128, 8, 48])
            self.stt("dve", self.ADAT[:, 8:16, :], self.ADAT[:, 8:16, :], 1.0, gm, ALU.add, ALU.mult,
                     [("ada", m) for m in range(8, 16)] + ["PF"], [("ada", m) for m in range(8, 16)])
            return
        gf = self.PF[:, PF_GF:PF_GF + 8].unsqueeze(2).broadcast_to([128, 8, 48])
        self.stt("dve", self.ADAT[:, 32:40, :], self.ADAT[:, 32:40, :], 1.0, gf, ALU.add, ALU.mult,
                 [("ada", m) for m in range(32, 40)] + ["PF"], [("ada", m) for m in range(32, 40)])
        idf = self.CST[:, 0:128]
        for (base, BC, SBt, nm) in ((16, self.GT1BC, self.GT1SB, "GT1"), (40, self.GT2BC, self.GT2SB, "GT2")):
            for half in range(2):
                b = P.bank()
                self.tr([(self.PS[b][:, j * 128:(j + 1) * 128],
                          self.ADAT[:, base + half * 4 + j, 0:1].broadcast_to([128, 128]), idf) for j in range(4)],
                        [("ada", base + half * 4 + j) for j in range(4)] + ["CST"], [("ps", b)])
                self.copy("dve", BC[:, half * 512:(half + 1) * 512], self.PS[b][:, :], [("ps", b)], [nm + "BC"])
                b = P.bank()
                self.tr([(self.PS[b][0:48, j * 128:(j + 1) * 128], self.ADAT[:, base + half * 4 + j, :], idf)
                         for j in range(4)],
                        [("ada", base + half * 4 + j) for j in range(4)] + ["CST"], [("ps", b)])
                self.copy("dve", SBt[:, half * 512:(half + 1) * 512], self.PS[b][0:48, :], [("ps", b)], [nm + "SB"])

    def rstd(self, n, ssc):
        self.act(self.RS[0:n, ssc:ssc + 1], self.SS[0:n, ssc:ssc + 1], AF.Sqrt, [("ss", ssc), "EPSC"], [("rs", ssc)],
                 bias=self.EPSC[0:n, 0:1], scale=1.0 / D)
        self.P.op("dve", lambda e: e.reciprocal(out=self.RS[0:n, ssc:ssc + 1], in_=self.RS[0:n, ssc:ssc + 1]),
                  [("rs", ssc)], [("rs", ssc)])

    def norm1a(self, n, src, src_res, ssc):
        self.act(self.JK[0:n, :], src, AF.Square, src_res, ["JK", ("ss", ssc)], accum_out=self.SS[0:n, ssc:ssc + 1])
        self.act(self.RS[0:n, ssc:ssc + 1], self.SS[0:n, ssc:ssc + 1], AF.Sqrt, [("ss", ssc), "EPSC"], [("rs", ssc)],
                 bias=self.EPSC[0:n, 0:1], scale=1.0 / D)

    def norm1b(self, n, src, src_res, ssc, xslot):
        self.P.op("dve", lambda e: e.reciprocal(out=self.RS[0:n, ssc:ssc + 1], in_=self.RS[0:n, ssc:ssc + 1]),
                  [("rs", ssc)], [("rs", ssc)])
        self.ts("dve", self.XN[0:n, xslot, :], src, self.RS[0:n, ssc:ssc + 1], None, ALU.mult, None,
                src_res + [("rs", ssc)], [("xn", xslot)])

    def norm1(self, n, src, src_res, ssc, xslot):
        self.norm1a(n, src, src_res, ssc)
        self.norm1b(n, src, src_res, ssc, xslot)

    def norm2(self, n, xslot, col0, sh_base, g_base, with_samples, ndve=4):
        P = self.P
        na = 8 - ndve
        ba = P.bank()
        bb = P.bank()
        pva = self.psb(ba)
        pvb = self.psb(bb)
        pj = lambda j: (pva[:, j * 128:j * 128 + n] if j < na else pvb[:, (j - na) * 128:(j - na) * 128 + n])
        self.tr([(pj(j), self.XN[0:n, xslot, j * 128:(j + 1) * 128], self.IDB[0:n, 0:n]) for j in range(na)],
                [("xn", xslot), "IDB"], [("ps", ba)])
        self.tr([(pj(j), self.XN[0:n, xslot, j * 128:(j + 1) * 128], self.IDB[0:n, 0:n]) for j in range(na, 8)],
                [("xn", xslot), "IDB"], [("ps", bb)])
        sg = segs(col0, col0 + n)
        for j in range(8):
            if j < na:
                self.act(self.HT[:, j, col0:col0 + n], pj(j), AF.Identity,
                         [("ps", ba), ("ada", sh_base + j), ("ada", g_base + j), "WDdone"], [("h", j, s) for s in sg],
                         bias=self.ADAT[:, sh_base + j, 0:1], scale=self.ADAT[:, g_base + j, 0:1])
            else:
                self.ts("dve", self.HT[:, j, col0:col0 + n], pj(j), self.ADAT[:, g_base + j, 0:1], self.ADAT[:, sh_base + j, 0:1],
                        ALU.mult, ALU.add, [("ps", bb), ("ada", sh_base + j), ("ada", g_base + j), "WDdone"], [("h", j, s) for s in sg])
        if with_samples:
            o0 = C_SMP - col0
            tmp = self.SCR[:, 6656:6784].rearrange("p (j t) -> p j t", j=8)
            pa3 = pva[:, 0:na * 128].rearrange("p (j t) -> p j t", j=na)[:, :, o0:o0 + 16]
            pb3 = pvb[:, 0:ndve * 128].rearrange("p (j t) -> p j t", j=ndve)[:, :, o0:o0 + 16]
            self.copy("act", tmp[:, 0:na, :], pa3, [("ps", ba)], ["scr_smp"])
            self.copy("dve", tmp[:, na:8, :], pb3, [("ps", bb)], ["scr_smp"])
            self.tt("dve", tmp, tmp, self.ADAT[:, g_base:g_base + 8, 32:48], ALU.mult,
                    ["scr_smp"] + [("ada", g_base + j) for j in range(8)], ["scr_smp"])
            self.tt("dve", self.HT[:, :, C_SMP:C_SMP + 16], tmp, self.ADAT[:, sh_base:sh_base + 8, 32:48], ALU.add,
                    ["scr_smp"] + [("ada", sh_base + j) for j in range(8)], [("h", j, 2) for j in range(8)])

    def norm_transpose(self, n, src, src_res, ssc, xslot, col0, sh_base, g_base, with_samples):
        self.norm1(n, src, src_res, ssc, xslot)
        self.norm2(n, xslot, col0, sh_base, g_base, with_samples)

    def p2_h1(self, h):
        blocks = []
        if h == 0:
            blocks += [(0, 128, 0), (128, 128, 128), (256, 48, 256)]
            blocks += [(C_OWN + 128 * b, 128, C_OWN + 128 * b) for b in range(8)]
        else:
            blocks += [(C_OWN + 128 * b, 128, NCOL + 128 * b) for b in range(8)]
        q1 = q2 = None
        for i, (col0, n, row0) in enumerate(blocks):
            xs = i % 2
            self.ld("xr%d" % xs, self.XR[0:n, xs, :], self.x_d[row0:row0 + n, :], writes=[("xr", xs)])
            ssc = self.ss_col
            self.ss_col = (self.ss_col + 1) % 64
            if q2 is not None:
                self.norm2(*q2)
                q2 = None
            if q1 is not None:
                self.norm1b(*q1[0])
                q2 = q1[1]
            self.norm1a(n, self.XR[0:n, xs, :], [("xr", xs)], ssc)
            q1 = ((n, self.XR[0:n, xs, :], [("xr", xs)], ssc, xs), (n, xs, col0, 0, 8, (h == 0 and col0 == 256)))
        if q2 is not None:
            self.norm2(*q2)
        self.norm1b(*q1[0])
        self.norm2(*q1[1])

    def ntiles(self, h, small0):
        t = []
        if h == 0:
            t.append((small0, 302 - small0))
        t += [(C_OWN, 512), (C_OWN + 512, 512)]
        return t

    def p3_qkv(self, h):
        P = self.P
        HT, KT = self.HT, self.KT
        mres_wv = [("M", c, s) for c in (22, 23) for s in range(11)]
        self.P.dma("pool", "wvk", lambda e: e.dma_start(out=self.WV, in_=self.wv_d.rearrange("(k p) n -> p k n", p=128)),
                   writes=mres_wv)
        WK2 = self.SCR[:, 0:1024].bitcast(BF16).rearrange("p (k n) -> p k n", k=8)
        self.P.dma("pool", "wvk", lambda e: e.dma_start(out=WK2, in_=self.wk_d.rearrange("(k p) n -> p k n", p=128)),
                   writes=["WK2"])
        self.P.end_group("wvk")
        import os
        skip = os.environ.get("K_SKIP", "").split(",")
        if h == 1 and "carry" not in skip:
            self.copy("dve", KT[:, :, 128:256], self.KCAR[:, :, :], ["KCAR", "WDdone"], [("k", c, 1) for c in range(2)])
            self.copy("dve", self.VT[:, 1, :], self.VCAR[:, :], ["VCAR", "WDdone"], [("v", 1)])
        ktiles = ([(0, 304)] if h == 0 else []) + [(C_OWN, 512), (C_OWN + 512, 512)]
        for c in range(2):
            w, wr = self.w_use(self.wp(h, c))
            for (c0, n) in ktiles:
                b = P.bank()
                sg = segs(c0, c0 + n)
                self.mm([(self.PS[b][:, 0:n], w[:, k, :], HT[:, k, c0:c0 + n]) for k in range(8)],
                        [wr] + [("h", k, s) for k in range(8) for s in sg], [("ps", b)])
                self.copy("act", KT[:, c, c0:c0 + n], self.PS[b][:, 0:n], [("ps", b), "WDdone"], [("k", c, s) for s in sg])
        for qc in range(8):
            w, wr = self.w_use(self.wp(h, 2 + qc))
            for (c0, n) in self.ntiles(h, C_SB):
                b = P.bank()
                sg = segs(c0, c0 + n)
                self.mm([(self.PS[b][:, 0:n], w[:, k, :], HT[:, k, c0:c0 + n]) for k in range(8)],
                        [wr] + [("h", k, s) for k in range(8) for s in sg], [("ps", b)])
                self.act(self.MREG[:, qc, c0 - MC0:c0 - MC0 + n], self.PS[b][:, 0:n], AF.Copy, [("ps", b)],
                         [("M", qc, s) for s in sg], scale=0.125)
        vblocks = ([(0, 0), (1, 128)] if h == 0 else []) + [(2 + lb, C_OWN + 128 * lb) for lb in range(8)]
        for (vb, c0) in vblocks:
            b = P.bank()
            sg = segs(c0, c0 + 128)
            self.mm([(self.PS[b][:, 0:256], HT[:, k, c0:c0 + 128], self.WV[:, k, :]) for k in range(8)],
                    mres_wv + [("h", k, s) for k in range(8) for s in sg], [("ps", b)])
            self.copy("act", self.VT[:, vb, :], self.PS[b][:, 0:256], [("ps", b), "WDdone"], [("v", vb)])
            if h == 1 and vb == 9 and "last" not in skip:
                if "lastv" not in skip:
                    VL = self.SCR[:, 1024:1280]
                    self.copy("act", VL, self.PS[b][:, 0:256], [("ps", b)], ["VL"])
                    if "nostore" not in skip:
                        src = self.VT[:, 9, :] if "altsrc" in skip else VL
                        dst = self.nkp_d if "altdst" in skip else self.nvp_d
                        self.st("fin", dst[:, :], VL, reads=["VL"])
                if "lastk" not in skip:
                    b2 = P.bank()
                    self.mm([(self.PS[b2][:, 0:256], HT[:, k, c0:c0 + 128], WK2[:, k, :]) for k in range(8)],
                            ["WK2"] + [("h", k, s) for k in range(8) for s in sg], [("ps", b2)])
                    KL = self.SCR[:, 1280:1536]
                    self.copy("dve", KL, self.PS[b2][:, 0:256], [("ps", b2)], ["KL"])
                    self.st("fin", self.nkp_d[:, :], KL, reads=["KL"])
        if h == 0:
            self.copy("dve", self.KCAR[:, :, :], KT[:, :, NCOL - 128:NCOL], [("k", c, 10) for c in range(2)], ["KCAR"])
            self.copy("dve", self.VCAR[:, :], self.VT[:, 9, :], [("v", 9)], ["VCAR"])
            b = P.bank()
            self.mm_multi([[(self.PS[b][0:16, 0:256], HT[:, k, C_SMP:C_SMP + 16], self.WV[:, k, :]) for k in range(8)],
                           [(self.PS[b][0:16, 256:512], HT[:, k, C_SMP:C_SMP + 16], WK2[:, k, :]) for k in range(8)]],
                          mres_wv + ["WK2"] + [("h", k, 2) for k in range(8)], [("ps", b)])
            VKS = self.SCR[0:16, 1536:2048]
            self.copy("dve", VKS, self.PS[b][0:16, :], [("ps", b)], ["VKS"])
            self.st("fin", self.nvs_d[:, 127, :], VKS[:, 0:256], reads=["VKS"])
            self.st("fin", self.nks_d[:, 127, :], VKS[:, 256:512], reads=["VKS"])
            self.st("x1w0", self.vs_d[:, :], VKS[:, 0:256], reads=["VKS"], writes=["vs_d"], final=False)

    def attn_block(self, qc0, nq, qpos, prev, cur):
        P = self.P
        KT, VT, M = self.KT, self.VT, self.MREG
        ql = qc0 - MC0
        qsg = segs(qc0, qc0 + nq)
        EB = self.SCR[:, 2048:3072].bitcast(BF16).rearrange("p (r x) -> p r x", r=4)
        DEN = self.SCR[:, 3072:4096].rearrange("p (r x) -> p r x", r=2)
        for c in range(2):
            bn = P.bank()
            bd = P.bank()
            groups = []
            gres = []
            for par in range(2):
                kv = 2 * c + par
                r0 = par * 64
                rows = slice(r0, r0 + 64)
                es = []
                for kb, (kc0, vblk, mask) in enumerate((prev, cur)):
                    ei = par * 2 + kb
                    b = P.bank()
                    ksg = segs(kc0, kc0 + 128)
                    S = self.PS[b][:, 0:4 * nq].rearrange("p (g q) -> p g q", g=4)
                    self.mm([(S, KT[rows, c, kc0:kc0 + 128], M[rows, 4 * c:4 * c + 4, ql:ql + nq]),
                             (S, self.IDB[:, :], mask[:, :, qpos:qpos + nq])],
                            [("k", c, s) for s in ksg] + [("M", 4 * c + g, s) for g in range(4) for s in qsg] + ["IDB", "masks"],
                            [("ps", b)])
                    E = EB[:, ei, 0:4 * nq].rearrange("p (g q) -> p g q", g=4)
                    self.act(E, S, AF.Exp, [("ps", b)], [("eb", ei)])
                    es.append((ei, E, vblk))
                NUM = self.PS[bn][rows, 0:4 * nq].rearrange("p (g q) -> p g q", g=4)
                DN = self.PS[bd][rows, 0:4 * nq].rearrange("p (g q) -> p g q", g=4)
                groups.append([(NUM, VT[:, vblk, kv * 64:(kv + 1) * 64], E) for (ei, E, vblk) in es])
                groups.append([(DN, self.ONESB[:, 0:64], E) for (ei, E, vblk) in es])
                gres += [("eb", ei) for (ei, _, _) in es] + [("v", vblk) for (_, _, vblk) in es]
            self.mm_multi(groups, gres + ["ONESB"], [("ps", bn), ("ps", bd)])
            dslot = c
            NUMA = self.PS[bn][:, 0:4 * nq].rearrange("p (g q) -> p g q", g=4)
            DNA = self.PS[bd][:, 0:4 * nq].rearrange("p (g q) -> p g q", g=4)
            DS = DEN[:, dslot, 0:4 * nq].rearrange("p (g q) -> p g q", g=4)
            sk = self.SINKP[:, c, :].unsqueeze(2).broadcast_to([128, 4, nq])
            self.tt("dve", DS, DNA, sk, ALU.add, [("ps", bd), "SINKP"], [("den", dslot)])
            self.act(DS, DS, AF.Ln, [("den", dslot)], [("den", dslot)])
            self.act(DS, DS, AF.Exp, [("den", dslot)], [("den", dslot)], scale=-1.0)
            self.tt("dve", M[:, 8 + 4 * c:8 + 4 * c + 4, ql:ql + nq], NUMA, DS, ALU.mult,
                    [("ps", bn), ("den", dslot)], [("M", 8 + 4 * c + g, s) for g in range(4) for s in qsg])

    def p4_attn(self, h, ada=False):
        P = self.P
        KT, VT, M = self.KT, self.VT, self.MREG
        EB = self.SCR[:, 2048:3072].bitcast(BF16).rearrange("p (r x) -> p r x", r=4)
        DEN = self.SCR[:, 3072:4096].rearrange("p (r x) -> p r x", r=2)
        blocks = []
        if h == 0:
            o = C_SB - MC0
            P.op("dve", lambda e: e.memset(self.MREG[:, 8:16, o:o + 50], 0.0), [],
                 [("M", 8 + q, s_) for q in range(8) for s_ in (1, 2)])
            blocks.append((C_SB, 2, 126, (0, 0, self.MPREV), (128, 1, self.MCUR)))
        for lb in range(8):
            qc0 = C_OWN + 128 * lb
            if lb == 0:
                prev = (128, 1, self.MPREV0 if h == 0 else self.MPREV)
            else:
                prev = (qc0 - 128, 1 + lb, self.MPREV)
            blocks.append((qc0, 128, 0, prev, (qc0, 2 + lb, self.MCUR)))
        steps = [(blk, kv) for blk in blocks for kv in range(4)]
        T = len(steps)
        st = {}

        def stage_A(t):
            (qc0, nq, qpos, prev, cur), kv = steps[t]
            c, par = kv // 2, kv % 2
            rows = slice(par * 64, par * 64 + 64)
            ql = qc0 - MC0
            qsg = segs(qc0, qc0 + nq)
            es = []
            for kb, (kc0, vblk, mask) in enumerate((prev, cur)):
                b = (2 * t + kb) % 4
                ei = (2 * t + kb) % 4
                ksg = segs(kc0, kc0 + 128)
                S = self.PS[b][:, 0:4 * nq].rearrange("p (g q) -> p g q", g=4)
                self.mm([(S, KT[rows, c, kc0:kc0 + 128], M[rows, 4 * c:4 * c + 4, ql:ql + nq]),
                         (S, self.IDB[:, :], mask[:, :, qpos:qpos + nq])],
                        [("k", c, s) for s in ksg] + [("M", 4 * c + g, s) for g in range(4) for s in qsg] + ["IDB", "masks"],
                        [("ps", b)])
                E = EB[:, ei, 0:4 * nq].rearrange("p (g q) -> p g q", g=4)
                self.act(E, S, AF.Exp, [("ps", b)], [("eb", ei)])
                es.append((ei, E, vblk))
            st[t] = es

        def stage_C(t):
            (qc0, nq, qpos, prev, cur), kv = steps[t]
            c, par = kv // 2, kv % 2
            rows = slice(par * 64, par * 64 + 64)
            u = t // 2
            bn = 4 + 2 * (u % 2)
            bd = bn + 1
            es = st.pop(t)
            NUM = self.PS[bn][rows, 0:4 * nq].rearrange("p (g q) -> p g q", g=4)
            DN = self.PS[bd][rows, 0:4 * nq].rearrange("p (g q) -> p g q", g=4)
            self.mm_multi([[(NUM, VT[:, vblk, kv * 64:(kv + 1) * 64], E) for (ei, E, vblk) in es],
                           [(DN, self.ONESB[:, 0:64], E) for (ei, E, vblk) in es]],
                          [("eb", ei) for (ei, _, _) in es] + [("v", vblk) for (_, _, vblk) in es] + ["ONESB"],
                          [("psr", bn, par), ("psr", bd, par)])

        def stage_D(u):
            (qc0, nq, qpos, prev, cur), kv = steps[2 * u]
            c = kv // 2
            bn = 4 + 2 * (u % 2)
            bd = bn + 1
            DNA = self.PS[bd][:, 0:4 * nq].rearrange("p (g q) -> p g q", g=4)
            DS = DEN[:, u % 2, 0:4 * nq].rearrange("p (g q) -> p g q", g=4)
            sk = self.SINKP[:, c, :].unsqueeze(2).broadcast_to([128, 4, nq])
            self.tt("dve", DS, DNA, sk, ALU.add, [("psr", bd, 0), ("psr", bd, 1), ("ps", bd), "SINKP"], [("den", u % 2)])

        def stage_EF(u):
            (qc0, nq, qpos, prev, cur), kv = steps[2 * u]
            c = kv // 2
            ql = qc0 - MC0
            qsg = segs(qc0, qc0 + nq)
            bn = 4 + 2 * (u % 2)
            NUMA = self.PS[bn][:, 0:4 * nq].rearrange("p (g q) -> p g q", g=4)
            DS = DEN[:, u % 2, 0:4 * nq].rearrange("p (g q) -> p g q", g=4)
            self.act(DS, DS, AF.Ln, [("den", u % 2)], [("den", u % 2)])
            self.act(DS, DS, AF.Exp, [("den", u % 2)], [("den", u % 2)], scale=-1.0)
            self.tt("dve", M[:, 8 + 4 * c:8 + 4 * c + 4, ql:ql + nq], NUMA, DS, ALU.mult,
                    [("psr", bn, 0), ("psr", bn, 1), ("ps", bn), ("den", u % 2)],
                    [("M", 8 + 4 * c + g, s) for g in range(4) for s in qsg])

        for t in range(T + 3):
            if ada and 2 <= t < 34:
                self.p1_ada(1, ms=[16 + t - 2], finish=False, bank=(2 * t + 2) % 4)
            if t == (T - 14 if ada else 2):
                self.P.dma("pool", "wo", lambda e: e.dma_start(out=self.WO, in_=self.wo_d.rearrange("(k p) n -> p k n", p=128)),
                           reads=["WDdone"], writes=["WO"])
            if h == 0 and t == (T - 10 if ada else 4):
                self.p4s_loads()
            if ada and t == T - 6:
                self.cache_copies()
            if t < T:
                stage_A(t)
            if 1 <= t <= T:
                stage_C(t - 1)
                if (t - 1) % 2 == 1:
                    stage_D((t - 1) // 2)
            if t >= 3 and (t - 3) % 2 == 0 and (t - 3) // 2 < T // 2:
                pass
            if t >= 3 and (t - 3) % 2 == 0:
                u = (t - 3) // 2
                if u < T // 2:
                    stage_EF(u)
        if ada:
            assert T >= 34
            self.psr_sync()
            self.p1_ada(1, ms=[], finish=True)

    def p4s_loads(self):
        P = self.P
        KC = self.SCR[:, 4096:6144].bitcast(BF16).rearrange("p (b f) -> p b f", b=16)
        P.dma("pool", "p4s", lambda e: e.dma_start(out=KC, in_=self.ck_d.rearrange("b s f -> s b f")), writes=["KC"])
        P.dma("pool", "p4s", lambda e: e.dma_start(out=self.VC, in_=self.cv_d.rearrange("b s f -> s b f")),
              writes=["VC"])
        VFL = self.SCR[0:1, 0:2048].bitcast(BF16)
        P.dma("pool", "p4s", lambda e: e.dma_start(out=VFL, in_=self.vs_d.rearrange("b f -> (b f)").unsqueeze(0)),
              reads=["vs_d"], writes=["VFL", "WK2"])
        P.end_group("p4s")

    def p4_samples(self):
        P = self.P
        KT, M = self.KT, self.MREG
        KC = self.SCR[:, 4096:6144].bitcast(BF16).rearrange("p (b f) -> p b f", b=16)
        VFL = self.SCR[0:1, 0:2048].bitcast(BF16)
        for c in range(2):
            for bh in range(2):
                b = P.bank()
                pv = self.psb(b)
                self.tr([(pv[:, i * 128:(i + 1) * 128], KC[:, bh * 8 + i, c * 128:(c + 1) * 128], self.IDB[:, :])
                         for i in range(8)], ["KC", "IDB"], [("ps", b)])
                self.copy("act", self.KCT[:, c, bh * 8:bh * 8 + 8, :].rearrange("p b s -> p (b s)"), pv[:, :],
                          [("ps", b)], ["KCT"])
        bS = P.bank()
        bN = P.bank()
        qs = C_SMP - MC0
        groups = []
        for b_ in range(16):
            for kv in range(4):
                c, par = kv // 2, kv % 2
                rows = slice(par * 64, par * 64 + 64)
                rhs = M[rows, 4 * c:4 * c + 4, qs + b_:qs + b_ + 1]
                o = (b_ * 4 + kv) * 4
                groups.append([(self.PS[bS][:, o:o + 4].unsqueeze(2), self.KCT[rows, c, b_, :], rhs)])
                groups.append([(self.PS[bN][0:1, o:o + 4].unsqueeze(2), KT[rows, c, C_SMP + b_:C_SMP + b_ + 1], rhs)])
        self.mm_multi(groups, ["KCT"] + [("k", c, 2) for c in range(2)] + [("M", q, 2) for q in range(8)],
                      [("ps", bS), ("ps", bN)])
        ES = self.SCR[:, 2048:2176].bitcast(BF16)
        EN = self.SCR[0:1, 2176:2304].bitcast(BF16)
        self.act(ES, self.PS[bS][:, 0:256], AF.Exp, [("ps", bS)], ["ES"])
        self.act(EN, self.PS[bN][0:1, 0:256], AF.Exp, [("ps", bN)], ["EN"])
        bO = P.bank()
        bD = P.bank()
        groups = []
        for b_ in range(16):
            for kv in range(4):
                c, par = kv // 2, kv % 2
                r0 = par * 64
                o = (b_ * 4 + kv) * 4
                oo = (b_ * 2 + c) * 4
                out = self.PS[bO][r0:r0 + 64, oo:oo + 4]
                groups.append([(out, self.VC[:, b_, kv * 64:(kv + 1) * 64], ES[:, o:o + 4]),
                               (out, VFL[0:1, b_ * 256 + kv * 64:b_ * 256 + kv * 64 + 64], EN[0:1, o:o + 4])])
        groups.append([(self.PS[bD][:, 0:256], self.ONESB[:, :], ES[:, :]),
                       (self.PS[bD][:, 0:256], self.ONESB[0:1, :], EN[0:1, :])])
        self.mm_multi(groups, ["VC", "VFL", "ES", "EN", "ONESB"], [("ps", bO), ("ps", bD)])
        DENS = self.SCR[:, 2304:2560]
        self.tt("dve", DENS.rearrange("p (b k) -> p b k", b=16), self.PS[bD][:, 0:256].rearrange("p (b k) -> p b k", b=16),
                self.SINKE[:, :].unsqueeze(1).broadcast_to([128, 16, 16]), ALU.add, [("ps", bD), "SINKE"], ["DENS"])
        self.P.op("dve", lambda e: e.reciprocal(out=DENS, in_=DENS), ["DENS"], ["DENS"])
        for par in range(2):
            r0 = par * 64
            rows = slice(r0, r0 + 64)
            num = self.PS[bO][rows, 0:128].rearrange("p (b c g) -> p c g b", b=16, c=2)
            den = DENS[rows, :].rearrange("p (b c q g) -> p c q g b", b=16, c=2, q=2)[:, :, par]
            out = M[rows, 8:16, qs:qs + 16].rearrange("p (c g) b -> p c g b", c=2)
            self.tt("dve", out, num, den, ALU.mult, [("ps", bO), "DENS"], [("M", 8 + q, 2) for q in range(8)])

    def p5_conv(self, h):
        P = self.P
        HT, M = self.HT, self.MREG
        tiles = self.ntiles(h, 252)
        XIN = self.SCR[:, 0:1024].rearrange("p (r x) -> p r x", r=2)
        U = self.SCR[:, 1024:2052].rearrange("p (r x) -> p r x", r=2)
        TC = self.SCR[:, 2052:3076].rearrange("p (r x) -> p r x", r=2)
        cw = lambda i, j: self.PF[:, PF_CW + i * 8 + j:PF_CW + i * 8 + j + 1]
        it = 0
        first_d = True
        for j in range(8):
            wx, wxr = self.w_use(self.wp(h, 10 + 3 * j))
            wc, wcr = self.w_use(self.wp(h, 10 + 3 * j + 1))
            wb, wbr = self.w_use(self.wp(h, 10 + 3 * j + 2))
            for ti, (c0, n) in enumerate(tiles):
                small = (h == 0 and ti == 0)
                big_i = ti - (1 if h == 0 else 0)
                sg = segs(c0, c0 + n)
                hres = [("h", k, s) for k in range(8) for s in sg]
                sl = it % 2
                it += 1
                b1, b2, b3 = P.bank(), P.bank(), P.bank()
                self.mm([(self.PS[b1][:, 0:n], wx[:, k, :], HT[:, k, c0:c0 + n]) for k in range(8)], [wxr] + hres, [("ps", b1)])
                self.mm([(self.PS[b2][:, 0:n], wc[:, k, :], HT[:, k, c0:c0 + n]) for k in range(8)], [wcr] + hres, [("ps", b2)])
                if small:
                    self.mm([(self.PS[b3][:, 0:48], wb[:, k, :], HT[:, k, C_SB:C_SB + 48]) for k in range(8)], [wbr] + hres, [("ps", b3)])
                    self.copy("act", XIN[:, sl, 0:n], self.PS[b1][:, 0:n], [("ps", b1)], [("xin", sl)])
                    self.tt("dve", self.USM[:, j, :], self.PS[b2][:, 0:n], XIN[:, sl, 0:n], ALU.mult,
                            [("ps", b2), ("xin", sl)], [("usm", j)])
                    self.copy("act", self.BSM[:, j, :], self.PS[b3][:, 0:48], [("ps", b3)], [("bsm", j)])
                    continue
                self.mm([(self.PS[b3][:, 0:n], wb[:, k, :], HT[:, k, c0:c0 + n]) for k in range(8)], [wbr] + hres, [("ps", b3)])
                self.copy("act", XIN[:, sl, 0:n], self.PS[b1][:, 0:n], [("ps", b1)], [("xin", sl)])
                if big_i == 0:
                    if h == 0:
                        self.ts("dve", U[:, sl, 0:2], self.USM[:, j, 2:4], self.PF[:, PF_FL:PF_FL + 1], None, ALU.mult, None,
                                [("usm", j), "PF"], [("u", sl)])
                    else:
                        self.copy("dve", U[:, sl, 0:2], self.ULAST[:, j, :], [("ulast", j)], [("u", sl)])
                else:
                    self.copy("dve", U[:, sl, 0:2], U[:, 1 - sl, 512:514], [("u", 1 - sl)], [("u", sl)])
                self.tt("dve", U[:, sl, 2:2 + n], self.PS[b2][:, 0:n], XIN[:, sl, 0:n], ALU.mult,
                        [("ps", b2), ("xin", sl)], [("u", sl)])
                if big_i == 1:
                    self.copy("dve", self.ULAST[:, j, :], U[:, sl, 512:514], [("u", sl)], [("ulast", j)])
                self.ts("dve", TC[:, sl, 0:n], U[:, sl, 0:n], cw(0, j), None, ALU.mult, None, [("u", sl), "PF"], [("tc", sl)])
                self.stt("dve", TC[:, sl, 0:n], U[:, sl, 1:1 + n], cw(1, j), TC[:, sl, 0:n], ALU.mult, ALU.add,
                         [("u", sl), ("tc", sl), "PF"], [("tc", sl)])
                self.stt("dve", TC[:, sl, 0:n], U[:, sl, 2:2 + n], cw(2, j), TC[:, sl, 0:n], ALU.mult, ALU.add,
                         [("u", sl), ("tc", sl), "PF"], [("tc", sl)])
                wres = [("M", 16 + j, s) for s in sg]
                if h == 0 and first_d:
                    wres = wres + ["KCT", "VC"]
                    first_d = False
                self.tt("dve", M[:, 16 + j, c0 - MC0:c0 - MC0 + n], self.PS[b3][:, 0:n], TC[:, sl, 0:n], ALU.mult,
                        [("ps", b3), ("tc", sl)], wres)
        if h == 0:
            self.p5_small()

    def p5_small(self):
        P = self.P
        M = self.MREG
        USM, BSM, CSM, STA = self.USM, self.BSM, self.CSM, self.STA
        ures = [("usm", j) for j in range(8)]
        ST = self.SCR[0:32, 3076:4100]
        self.ld("st", ST, self.sca_d.rearrange("b t f -> (b t) f"), writes=["ST"])
        b = P.bank()
        self.tr([(self.PS[b][:, j * 32:(j + 1) * 32], ST[:, j * 128:(j + 1) * 128], self.CST[0:32, 0:32]) for j in range(8)],
                ["ST", "CST"], [("ps", b)])
        self.copy("dve", STA[:, :, :].rearrange("p j x -> p (j x)"), self.PS[b][:, 0:256], [("ps", b)], ["STA"])
        wbc = lambda i, n: self.PF[:, PF_CW + i * 8:PF_CW + i * 8 + 8].unsqueeze(2).broadcast_to([128, 8, n])
        c2 = CSM[:, :, 0:2]
        self.tt("dve", c2, USM[:, :, 0:2], wbc(0, 2), ALU.mult, ures + ["PF"], ["csm"])
        t2 = self.SCR[:, 4100:4116].rearrange("p (j x) -> p j x", j=8)
        self.tt("dve", t2, USM[:, :, 1:3], wbc(1, 2), ALU.mult, ures + ["PF"], ["t2"])
        self.tt("dve", c2, c2, t2, ALU.add, ["csm", "t2"], ["csm"])
        self.tt("dve", t2, USM[:, :, 2:4], wbc(2, 2), ALU.mult, ures + ["PF", "csm"], ["t2"])
        self.tt("dve", c2, c2, t2, ALU.add, ["csm", "t2"], ["csm"])
        st3 = STA[:, :, :].rearrange("p j (b t) -> p j b t", t=2)
        cs = CSM[:, :, 32:48]
        t16 = self.SCR[:, 4116:4244].rearrange("p (j x) -> p j x", j=8)
        self.tt("dve", cs, st3[:, :, :, 0], wbc(0, 16), ALU.mult, ["STA", "PF", "csm"], ["csm"])
        self.tt("dve", t16, st3[:, :, :, 1], wbc(1, 16), ALU.mult, ["STA", "PF"], ["t16"])
        self.tt("dve", cs, cs, t16, ALU.add, ["csm", "t16"], ["csm"])
        self.tt("dve", t16, USM[:, :, 34:50], wbc(2, 16), ALU.mult, ures + ["PF", "csm"], ["t16"])
        self.tt("dve", cs, cs, t16, ALU.add, ["csm", "t16"], ["csm"])
        o = C_SB - MC0
        P.op("dve", lambda e: e.memset(M[:, 16:24, o:o + 48], 0.0), [], [("M", 16 + j, s) for j in range(8) for s in (1, 2)])
        self.tt("dve", M[:, 16:24, o:o + 2], BSM[:, :, 0:2], c2, ALU.mult, [("bsm", j) for j in range(8)] + ["csm"],
                [("M", 16 + j, 1) for j in range(8)])
        self.tt("dve", M[:, 16:24, o + 32:o + 48], BSM[:, :, 32:48], cs, ALU.mult, [("bsm", j) for j in range(8)] + ["csm"],
                [("M", 16 + j, 2) for j in range(8)])
        b = P.bank()
        self.tr([(self.PS[b][0:16, j * 128:(j + 1) * 128], USM[:, j, 34:50], self.CST[:, 0:128]) for j in range(4)],
                ures + ["CST"], [("ps", b)])
        b2 = P.bank()
        self.tr([(self.PS[b2][0:16, j * 128:(j + 1) * 128], USM[:, 4 + j, 34:50], self.CST[:, 0:128]) for j in range(4)],
                ures + ["CST"], [("ps", b2)])
        UST = self.SCR[0:16, 4244:5268]
        self.copy("dve", UST[:, 0:512], self.PS[b][0:16, :], [("ps", b)], ["UST"])
        self.copy("dve", UST[:, 512:1024], self.PS[b2][0:16, :], [("ps", b2)], ["UST"])
        self.st("fin", self.ncas_d[:, 1, :], UST, reads=["UST"])

    def p6_gates(self, h):
        P = self.P
        HT, M = self.HT, self.MREG
        tiles = self.ntiles(h, C_SB)
        SG = self.SCR[:, 0:2048].rearrange("p (r x) -> p r x", r=4)
        T1 = self.SCR[:, 2048:4096].rearrange("p (r x) -> p r x", r=4)
        it = 0
        for j in range(8):
            wa, war = self.w_use(self.wp(h, 34 + 4 * j))
            wb, wbr = self.w_use(self.wp(h, 34 + 4 * j + 1))
            wga, wgar = self.w_use(self.wp(h, 34 + 4 * j + 2))
            wgb, wgbr = self.w_use(self.wp(h, 34 + 4 * j + 3))
            for (c0, n) in tiles:
                sg = segs(c0, c0 + n)
                l0 = c0 - MC0
                sl = it % 2
                it += 1
                ba, bb, bga, bgb = P.bank(), P.bank(), P.bank(), P.bank()
                hres = [("h", k, s) for k in range(8) for s in sg]
                self.mm([(self.PS[bga][:, 0:n], wga[:, k, :], HT[:, k, c0:c0 + n]) for k in range(8)], [wgar] + hres, [("ps", bga)])
                self.mm([(self.PS[bgb][:, 0:n], wgb[:, k, :], HT[:, k, c0:c0 + n]) for k in range(8)], [wgbr] + hres, [("ps", bgb)])
                self.mm([(self.PS[ba][:, 0:n], wa[:, k, :], M[:, 16 + k, l0:l0 + n]) for k in range(8)],
                        [war] + [("M", 16 + k, s) for k in range(8) for s in sg], [("ps", ba)])
                self.mm([(self.PS[bb][:, 0:n], wb[:, k, :], M[:, 8 + k, l0:l0 + n]) for k in range(8)],
                        [wbr] + [("M", 8 + k, s) for k in range(8) for s in sg], [("ps", bb)])
                self.act(SG[:, sl * 2, 0:n], self.PS[bga][:, 0:n], AF.Sigmoid, [("ps", bga)], [("sg", sl * 2)])
                self.act(SG[:, sl * 2 + 1, 0:n], self.PS[bgb][:, 0:n], AF.Sigmoid, [("ps", bgb)], [("sg", sl * 2 + 1)])
                self.tt("dve", T1[:, sl * 2, 0:n], self.PS[ba][:, 0:n], SG[:, sl * 2, 0:n], ALU.mult,
                        [("ps", ba), ("sg", sl * 2)], [("t1", sl * 2)])
                self.tt("dve", T1[:, sl * 2 + 1, 0:n], self.PS[bb][:, 0:n], SG[:, sl * 2 + 1, 0:n], ALU.mult,
                        [("ps", bb), ("sg", sl * 2 + 1)], [("t1", sl * 2 + 1)])
                self.tt("dve", M[:, j, l0:l0 + n], T1[:, sl * 2, 0:n], T1[:, sl * 2 + 1, 0:n], ALU.add,
                        [("t1", sl * 2), ("t1", sl * 2 + 1)], [("M", j, s) for s in sg])

    def resid_block(self, n, c0, nk, W, wres, mbase, xsrc, gt_tile, gt_res, xslot, x1res, korder=None, pre=None):
        P = self.P
        M = self.MREG
        l0 = c0 - MC0
        sg = segs(c0, c0 + n)
        ns = getattr(self, "x1_slots", 2)
        X1 = self.SCR[:, 0:1024 * ns].rearrange("p (r x) -> p r x", r=ns)
        TT = self.SCR[:, 1024 * ns:1024 * ns + 1024].rearrange("p (r x) -> p r x", r=2)
        for half in range(2):
            if pre is not None:
                b = pre[half]
                self.mm([(self.PS[b][0:n, :], M[:, mbase + k, l0:l0 + n], W[:, k, half * 512:(half + 1) * 512]) for k in korder],
                        wres + [("M", mbase + k, s) for k in korder for s in sg], [("ps", b)], start=False, stop=True)
            else:
                b = P.bank()
                self.mm([(self.PS[b][0:n, :], M[:, mbase + k, l0:l0 + n], W[:, k, half * 512:(half + 1) * 512])
                         for k in (korder if korder is not None else range(nk))],
                        wres + [("M", mbase + k, s) for k in range(nk) for s in sg], [("ps", b)])
            self.tt("dve", TT[0:n, half, :], self.PS[b][0:n, :], gt_tile[0:n, half * 512:(half + 1) * 512], ALU.mult,
                    [("ps", b)] + gt_res, [("tt", half)])
            self.tt("dve", X1[0:n, xslot, half * 512:(half + 1) * 512], TT[0:n, half, :], xsrc[:, half * 512:(half + 1) * 512],
                    ALU.add, [("tt", half)] + x1res, [("x1", xslot)])
        return X1[0:n, xslot, :]

    def p7_wo(self, h):
        P = self.P
        X1S = self.GT1SB
        self.x1_slots = 3
        blocks = []
        if h == 0:
            blocks.append(("small", C_SB, 48, C_SB))
        blocks += [("own", C_OWN + 128 * lb, 128, (C_OWN if h == 0 else NCOL) + 128 * lb) for lb in range(8)]
        qa = []
        q2 = None
        for it, (kind, c0, n, row0) in enumerate(blocks):
            xs = it % 2
            x3 = it % 3
            if q2 is not None:
                self.norm2(*q2)
                q2 = None
            if len(qa) == 2:
                a = qa.pop(0)
                self.norm1b(*a[0])
                q2 = a[1]
            if it == 0:
                self.ld("xr%d" % xs, self.XR[0:n, xs, :], self.x_d[row0:row0 + n, :], writes=[("xr", xs)])
            if it + 1 < len(blocks):
                (_, _, n2, row2) = blocks[it + 1]
                xs2 = (it + 1) % 2
                self.ld("xr%d" % xs2, self.XR[0:n2, xs2, :], self.x_d[row2:row2 + n2, :], writes=[("xr", xs2)])
            gt = self.GT1SB if kind == "small" else self.GT1BC
            gres = ["GT1SB"] if kind == "small" else ["GT1BC"]
            x1 = self.resid_block(n, c0, 8, self.WO, ["WO"], 0, self.XR[0:n, xs, :], gt, gres, x3, [("xr", xs)])
            ssc = self.ss_col
            self.ss_col = (self.ss_col + 1) % 64
            if kind == "small":
                self.copy("act", X1S[:, :], x1, [("x1", x3)], ["GT1SB"])
            else:
                lb = (c0 - C_OWN) // 128
                r = h * 1024 + lb * 128
                self.st("x1w%d" % x3, self.x1_d[r:r + 128, :], x1, reads=[("x1", x3)], writes=[("x1d", h, lb)], final=False)
            self.norm1a(n, x1, [("x1", x3)], ssc)
            qa.append(((n, x1, [("x1", x3)], ssc, xs), (n, xs, c0, 24, 32, (kind == "small"), 2)))
        if q2 is not None:
            self.norm2(*q2)
        for a in qa:
            self.norm1b(*a[0])
            self.norm2(*a[1])
        self.x1_slots = 2

    def p8_ffn(self, h):
        P = self.P
        HT, M = self.HT, self.MREG
        tiles = self.ntiles(h, C_SB)
        UP = self.SCR[:, 0:2056].rearrange("p (r x) -> p r x", r=4)
        TG = self.SCR[:, 2056:4104].rearrange("p (r x) -> p r x", r=4)
        SGL = self.SCR[:, 4104:5128].rearrange("p (r x) -> p r x", r=2)
        UPSM = self.SCR[:, 5128:5128 + 792].rearrange("p (c x) -> p c x", c=44)
        self.UPSM = UPSM
        wdv = self.wd_d.rearrange("(k p) n -> p k n", p=128)
        eres_hi = [("k", c, s_) for c in range(2) for s_ in range(11)] + [("v", i) for i in range(10)] + ["WO", "KCARdummy"]
        for i, (k0, k1) in enumerate(((11, 17), (17, 22))):
            self.P.dma("pool", "wd", lambda e, k0=k0, k1=k1: e.dma_start(out=self.WD[:, k0:k1, :], in_=wdv[:, k0:k1, :]),
                       writes=[("WD", 2 + i)] + (eres_hi if i == 0 else []))
        self.P.end_group("wd")
        fw = lambda i, c: self.PF[:, PF_FW + i * 44 + c:PF_FW + i * 44 + c + 1]
        fb = lambda c: self.PF[:, PF_FB + c:PF_FB + c + 1]
        it = 0
        tail = None
        for p in range(22):
            chunks = (p, 22 + p)
            ws = [self.w_use(self.wp(h, 66 + 2 * p)), self.w_use(self.wp(h, 66 + 2 * p + 1))]
            for ti, (c0, n) in enumerate(tiles):
                small = (h == 0 and ti == 0)
                big_i = ti - (1 if h == 0 else 0)
                sg = segs(c0, c0 + n)
                hres = [("h", k, s) for k in range(8) for s in sg]
                sl = it % 2
                if not small:
                    it += 1
                tres = []
                for hi, ch in enumerate(chunks):
                    w, wr = ws[hi]
                    b = P.bank()
                    us = sl * 2 + hi
                    self.mm([(self.PS[b][:, 0:n], w[:, k, :], HT[:, k, c0:c0 + n]) for k in range(8)], [wr] + hres, [("ps", b)])
                    if small:
                        self.copy("act", UPSM[:, ch, 0:2], self.PS[b][:, 0:2], [("ps", b)], [("upsm", ch)])
                        self.copy("act", UPSM[:, ch, 2:18], self.PS[b][:, 32:48], [("ps", b)], [("upsm", ch)])
                        continue
                    self.copy("act", UP[:, us, 2:2 + n], self.PS[b][:, 0:n], [("ps", b)], [("up", us)])
                    self.act(TG[:, us, 0:n], self.PS[b][:, 0:n], AF.Identity, [("ps", b), "PF"], [("tg", us)],
                             bias=fb(ch), scale=fw(2, ch))
                    if big_i == 0:
                        if h == 0:
                            self.ts("pool", UP[:, us, 0:2], UPSM[:, ch, 0:2], self.PF[:, PF_FL:PF_FL + 1], None, ALU.mult, None,
                                    [("upsm", ch), "PF"], [("up", us)])
                        else:
                            self.copy("pool", UP[:, us, 0:2], self.UPLAST[:, ch, :], [("uplast", ch)], [("up", us)])
                    else:
                        self.copy("pool", UP[:, us, 0:2], UP[:, (1 - sl) * 2 + hi, 512:514], [("up", (1 - sl) * 2 + hi)], [("up", us)])
                    if big_i == 1:
                        self.copy("pool", self.UPLAST[:, ch, :], UP[:, us, 512:514], [("up", us)], [("uplast", ch)])
                    self.stt("dve", TG[:, us, 0:n], UP[:, us, 1:1 + n], fw(1, ch), TG[:, us, 0:n], ALU.mult, ALU.add,
                             [("up", us), ("tg", us), "PF"], [("tg", us)])
                    self.stt("dve", TG[:, us, 0:n], UP[:, us, 0:n], fw(0, ch), TG[:, us, 0:n], ALU.mult, ALU.add,
                             [("up", us), ("tg", us), "PF"], [("tg", us)])
                    tres.append(("tg", us))
                if small:
                    continue
                if tail is not None:
                    self.p8_tail(*tail)
                tail = (SGL, TG, M, sl, n, p, c0, sg)
        if tail is not None:
            self.p8_tail(*tail)
        self.scr_switch()
        self.wd_low_loads()
        if h == 0:
            self.p8_samples()

    def p8_tail(self, SGL, TG, M, sl, n, p, c0, sg):
        self.act(SGL[:, sl, 0:n], TG[:, sl * 2, 0:n], AF.Silu, [("tg", sl * 2)], [("sgl", sl)])
        self.tt("pool", M[:, p, c0 - MC0:c0 - MC0 + n], SGL[:, sl, 0:n], TG[:, sl * 2 + 1, 0:n], ALU.mult,
                [("sgl", sl), ("tg", sl * 2 + 1)], [("M", p, s) for s in sg])

    def p8_samples(self):
        P = self.P
        M = self.MREG
        UPSM = self.UPSM
        ures = [("upsm", c) for c in range(44)]
        STF = self.SCR[:, 0:1408].rearrange("p (c x) -> p c x", c=44)
        STT = self.SCR[0:32, 1408:2816]
        sffv = self.sff_d.rearrange("b t f -> (b t) f")
        for q in range(4):
            self.ld("st", STT, sffv[:, q * 1408:(q + 1) * 1408], writes=["STT"])
            for g0 in range(0, 11, 8):
                ng = min(8, 11 - g0)
                b = P.bank()
                self.tr([(self.PS[b][:, i * 32:(i + 1) * 32], STT[:, (g0 + i) * 128:(g0 + i + 1) * 128], self.CST[0:32, 0:32])
                         for i in range(ng)], ["STT", "CST"], [("ps", b)])
                self.copy("dve", STF[:, q * 11 + g0:q * 11 + g0 + ng, :].rearrange("p c x -> p (c x)"),
                          self.PS[b][:, 0:ng * 32], [("ps", b)], ["STF"])
        st4 = STF[:, :, :].rearrange("p c (b t) -> p c b t", t=2)
        CV = self.SCR[:, 2816:3520].rearrange("p (c x) -> p c x", c=44)
        T = self.SCR[:, 3520:4224].rearrange("p (c x) -> p c x", c=44)
        wbc = lambda i: self.PF[:, PF_FW + i * 44:PF_FW + i * 44 + 44].unsqueeze(2).broadcast_to([128, 44, 16])
        bbc = self.PF[:, PF_FB:PF_FB + 44].unsqueeze(2).broadcast_to([128, 44, 16])
        self.tt("dve", CV, st4[:, :, :, 0], wbc(0), ALU.mult, ["STF", "PF"], ["CV"])
        self.tt("dve", T, st4[:, :, :, 1], wbc(1), ALU.mult, ["STF", "PF"], ["T44"])
        self.tt("dve", CV, CV, T, ALU.add, ["CV", "T44"], ["CV"])
        self.tt("dve", T, UPSM[:, :, 2:18], wbc(2), ALU.mult, ures + ["PF", "CV"], ["T44"])
        self.tt("dve", CV, CV, T, ALU.add, ["CV", "T44"], ["CV"])
        self.tt("dve", CV, CV, bbc, ALU.add, ["CV", "PF"], ["CV"])
        self.act(T[:, 0:22, :], CV[:, 0:22, :], AF.Silu, ["CV", "T44"], ["T44"])
        o = C_SMP - MC0
        self.tt("dve", M[:, 0:22, o:o + 16], T[:, 0:22, :], CV[:, 22:44, :], ALU.mult, ["T44", "CV"],
                [("M", c, 2) for c in range(22)])
        UT = self.SCR[0:16, 1408:2816]
        for q in range(4):
            for g0 in range(0, 11, 4):
                ng = min(4, 11 - g0)
                b = P.bank()
                self.tr([(self.PS[b][0:16, i * 128:(i + 1) * 128], UPSM[:, q * 11 + g0 + i, 2:18], self.CST[:, 0:128])
                         for i in range(ng)], ures + ["CST"], [("ps", b)])
                self.copy("dve", UT[:, g0 * 128:(g0 + ng) * 128], self.PS[b][0:16, 0:ng * 128], [("ps", b)], ["STT"])
            self.st("st", self.nfs_d[:, 1, q * 1408:(q + 1) * 1408], UT, reads=["STT"])

    def wd_low_loads(self):
        eres = [("h", k, s) for k in range(8) for s in range(11)]
        wdv = self.wd_d.rearrange("(k p) n -> p k n", p=128)
        for i, (k0, k1) in enumerate(((0, 6), (6, 11))):
            self.P.dma("pool", "wdl", lambda e, k0=k0, k1=k1: e.dma_start(out=self.WD[:, k0:k1, :], in_=wdv[:, k0:k1, :]),
                       writes=[("WD", i)] + (eres if i == 0 else []))
        self.P.end_group("wdl")

    def p9_down(self, h):
        P = self.P
        M = self.MREG
        X1S = self.GT1SB
        pend = None
        blocks = [("own", C_OWN + 128 * lb, 128) for lb in range(8)]
        if h == 0:
            blocks.append(("small", C_SB, 48))
        khigh = list(range(11, 22))
        klow = list(range(0, 11))
        pre = {}
        for bi in range(4):
            (kind, c0, n) = blocks[bi]
            l0 = c0 - MC0
            sg = segs(c0, c0 + n)
            bb = []
            for half in range(2):
                b = P.bank()
                self.mm([(self.PS[b][0:n, :], M[:, k, l0:l0 + n], self.WD[:, k, half * 512:(half + 1) * 512]) for k in khigh],
                        [("WD", 2), ("WD", 3)] + [("M", k, s_) for k in khigh for s_ in sg], [("ps", b)], start=True, stop=False)
                bb.append(b)
            pre[bi] = bb

        def load(i):
            (kind, c0, n) = blocks[i]
            if kind == "small":
                return
            xs = i % 2
            lb = (c0 - C_OWN) // 128
            r = h * 1024 + lb * 128
            self.ld("xr%d" % xs, self.XR[0:n, xs, :], self.x1_d[r:r + 128, :], reads=[("x1d", h, lb)], writes=[("xr", xs)])

        load(0)
        for i, (kind, c0, n) in enumerate(blocks):
            xs = i % 2
            if i + 1 < len(blocks):
                load(i + 1)
            if kind == "small":
                xsrc, xres = X1S[:, :], ["GT1SB"]
                gt, gres = self.GT2SB, ["GT2SB"]
            else:
                xsrc, xres = self.XR[0:n, xs, :], [("xr", xs)]
                gt, gres = self.GT2BC, ["GT2BC"]
            if i in pre:
                x2 = self.resid_block(n, c0, 22, self.WD, [("WD", 0), ("WD", 1)], 0, xsrc, gt, gres, xs, xres,
                                       korder=klow, pre=pre[i])
            else:
                x2 = self.resid_block(n, c0, 22, self.WD, [("WD", j) for j in range(4)], 0, xsrc, gt, gres, xs, xres,
                                       korder=khigh + klow)
            if pend is not None:
                self.p9_epilogue(*pend)
            pend = (h, kind, c0, n, x2, xs)
        self.p9_epilogue(*pend)
        bm = P.bank()
        self.P.op("pe", lambda e: e.matmul(self.PS[bm][0:1, 0:1], lhsT=self.ONESB[0:1, 0:1], rhs=self.ONESB[0:1, 0:1],
                                           start=True, stop=True), ["ONESB"], [("ps", bm), "WDdone"])

    def p9_epilogue(self, h, kind, c0, n, x2, xs):
        YB = self.SCR[:, 3072:5120].rearrange("p (r x) -> p r x", r=2)[:, xs, :]
        ssc = self.ss_col
        self.ss_col = (self.ss_col + 1) % 64
        self.act(self.JK[0:n, :], x2, AF.Square, [("x1", xs)], ["JK", ("ss", ssc)], accum_out=self.SS[0:n, ssc:ssc + 1])
        self.rstd(n, ssc)
        self.stt("dve", YB[0:n, :], x2, self.RS[0:n, ssc:ssc + 1], self.GFIN[0:n, :], ALU.mult, ALU.mult,
                 [("x1", xs), ("rs", ssc), "GFIN"], [("YB", xs)])
        if kind == "small":
            self.st("yw%d" % xs, self.ys_d[:, :], YB[32:48, :], reads=[("YB", xs)])
        else:
            lb = (c0 - C_OWN) // 128
            r = h * 1024 + lb * 128
            self.st("yw%d" % xs, self.y_d[r:r + 128, :], YB[0:128, :], reads=[("YB", xs)])

    def final_outputs(self):
        P = self.P
        b = P.bank()
        b2 = P.bank()
        self.tr([(self.PS[b][0:2, j * 128:(j + 1) * 128], self.ULAST[:, j, :], self.CST[:, 0:128]) for j in range(4)],
                [("ulast", j) for j in range(8)] + ["CST"], [("ps", b)])
        self.tr([(self.PS[b2][0:2, j * 128:(j + 1) * 128], self.ULAST[:, 4 + j, :], self.CST[:, 0:128]) for j in range(4)],
                [("ulast", j) for j in range(8)] + ["CST"], [("ps", b2)])
        UO = self.SCR[0:2, 0:1024]
        self.copy("dve", UO[:, 0:512], self.PS[b][0:2, :], [("ps", b)], ["UO"])
        self.copy("dve", UO[:, 512:1024], self.PS[b2][0:2, :], [("ps", b2)], ["UO"])
        self.st("fin", self.ncap_d[:, :], UO, reads=["UO"])
        FO = self.SCR[0:2, 1024:1024 + 5632]
        for g0 in range(0, 44, 4):
            b = P.bank()
            self.tr([(self.PS[b][0:2, i * 128:(i + 1) * 128], self.UPLAST[:, g0 + i, :], self.CST[:, 0:128]) for i in range(4)],
                    [("uplast", c) for c in range(44)] + ["CST"], [("ps", b)])
            self.copy("dve", FO[:, g0 * 128:(g0 + 4) * 128], self.PS[b][0:2, :], [("ps", b)], ["FO"])
        self.st("fin", self.nfp_d[:, :], FO, reads=["FO"])

    def dtap(self, name, ap, shape):
        if not self.debug:
            return
        self.P.barrier()
        d = self.nc.dram_tensor("dbg_" + name, list(shape), ap.dtype, kind="ExternalOutput").ap()
        self.taps.append(name)
        self.st("dbg_" + name, d, ap)

    def psr_sync(self):
        for b in range(4, 8):
            self.P.op("pe", lambda e, b=b: e.matmul(self.PS[b][0:1, 0:1], lhsT=self.ONESB[0:1, 0:1], rhs=self.ONESB[0:1, 0:1],
                                                    start=True, stop=True),
                      ["ONESB"], [("ps", b), ("psr", b, 0), ("psr", b, 1)])

    def scr_switch(self):
        self.P.phase_switch(lambda e: e.memset(self.TOK[:], 0.0))

    def reg_scr(self):
        A = self.P.add_alias
        A("CT", 0, 384); A("tmpm", 1024, 1536); A("scr_smp", 6656, 6784)
        A("WK2", 0, 1024); A("VL", 1024, 1280); A("KL", 1280, 1536); A("VKS", 1536, 2048)
        for i in range(4):
            A(("eb", i), 2048 + i * 256, 2048 + (i + 1) * 256)
            A(("sg", i), i * 512, (i + 1) * 512)
            A(("t1", i), 2048 + i * 512, 2048 + (i + 1) * 512)
            A(("up", i), i * 514, (i + 1) * 514)
            A(("tg", i), 2056 + i * 512, 2056 + (i + 1) * 512)
        for u in range(2):
            A(("den", u), 3072 + u * 512, 3072 + (u + 1) * 512)
            A(("xin", u), u * 512, (u + 1) * 512)
            A(("u", u), 1024 + u * 514, 1024 + (u + 1) * 514)
            A(("tc", u), 2052 + u * 512, 2052 + (u + 1) * 512)
            A(("x1", u), u * 1024, (u + 1) * 1024)
            A(("x1", 2), 2048, 3072)
            A(("tt", u), 2048 + u * 512, 2048 + (u + 1) * 512)
            A(("YB", u), 3072 + u * 1024, 3072 + (u + 1) * 1024)
            A(("sgl", u), 4104 + u * 512, 4104 + (u + 1) * 512)
        A("KC", 4096, 6144); A("VFL", 0, 2048); A("ES", 2048, 2176); A("EN", 2176, 2304); A("DENS", 2304, 2560)
        A("ST", 3076, 4100); A("t2", 4100, 4116); A("t16", 4116, 4244); A("UST", 4244, 5268)
        for j in range(8):
            A(("usm", j), 5268 + j * 50, 5268 + (j + 1) * 50)
            A(("bsm", j), 5668 + j * 48, 5668 + (j + 1) * 48)
        A("csm", 6052, 6436); A("STA", 6436, 6692)
        for ch in range(44):
            A(("upsm", ch), 5128 + ch * 18, 5128 + (ch + 1) * 18)
        A("STF", 0, 1408); A("STT", 1408, 2816); A("CV", 2816, 3520); A("T44", 3520, 4224)
        A("UO", 0, 1024); A("FO", 1024, 6656)

    def build(self):
        import os
        self.reg_scr()
        stop = int(os.environ.get("K_STOP", "99"))
        P = self.P
        self.w_init()
        if self.debug:
            P.op("dve", lambda e: e.memset(self.EREG[:, :], 0.0), [], ["dbgz"])
            P.op("dve", lambda e: e.memset(self.MREGF[:, :], 0.0), [], ["dbgz"])
            P.barrier()
        self.p0_consts()
        if stop >= 1:
            self.p1_ada(0)
        stop0 = stop
        stop1 = int(os.environ.get("K_STOP1", "99"))
        for h in self.halves:
            stop = stop0 if h == 0 else stop1
            if stop < 2:
                break
            if h == self.halves[0]:
                P.barrier()
            else:
                self.scr_switch()
            self.p2_h1(h)
            self.dtap("H1_%d" % h, self.EREG[:, 0:8 * NCOL], [128, 8 * NCOL])
            if stop < 3:
                break
            self.scr_switch()
            self.p3_qkv(h)
            self.dtap("KT_%d" % h, self.EREG[:, 8 * NCOL:10 * NCOL], [128, 2 * NCOL])
            self.dtap("VT_%d" % h, self.EREG[:, 10 * NCOL:10 * NCOL + 2560], [128, 2560])
            self.dtap("Q_%d" % h, self.MREGF[:, 0:8 * MW], [128, 8 * MW])
            if stop < 4:
                break
            self.scr_switch()
            self.psr_sync()
            self.p4_attn(h, ada=(h == self.halves[0]))
            self.psr_sync()
            self.dtap("ADAT", self.ADAT[:].rearrange("p a b -> p (a b)"), [128, 48 * 48])
            if h == 0 and stop != 4:
                self.scr_switch()
                self.p4_samples()
            self.dtap("ATT_%d" % h, self.MREGF[:, 8 * MW:16 * MW], [128, 8 * MW])
            if stop < 5:
                break
            self.scr_switch()
            self.p5_conv(h)
            self.dtap("BC_%d" % h, self.MREGF[:, 16 * MW:24 * MW], [128, 8 * MW])
            if stop < 6:
                break
            self.scr_switch()
            self.p6_gates(h)
            self.dtap("G_%d" % h, self.MREGF[:, 0:8 * MW], [128, 8 * MW])
            if stop < 7:
                break
            self.scr_switch()
            self.p7_wo(h)
            self.dtap("H2_%d" % h, self.EREG[:, 0:8 * NCOL], [128, 8 * NCOL])
            if stop < 8:
                break
            self.scr_switch()
            self.p8_ffn(h)
            self.dtap("MT_%d" % h, self.MREGF[:, 0:22 * MW], [128, 22 * MW])
            if stop < 9:
                break
            self.scr_switch()
            self.p9_down(h)
        if stop0 >= 10 and stop1 >= 10:
            P.barrier()
            self.final_outputs()
        self.stats = P.finalize_and_emit()
        return self.nc


def _chunks(W):
    K, N = W.shape
    return np.ascontiguousarray(W.reshape(K // 128, 128, N // 128, 128).transpose(2, 1, 0, 3))


def _qperm():
    idx = []
    for c in range(2):
        for g in range(4):
            for hh in (8 * c + g, 8 * c + 4 + g):
                idx.extend(range(hh * 64, hh * 64 + 64))
    return np.array(idx)


def _host_prep(inp):
    f = lambda a: np.ascontiguousarray(np.asarray(a, dtype=np.float32))
    w_in = f(inp["w_in"])[0]
    xin_w, b_w, c_w = w_in[:, 0:1024], w_in[:, 1024:2048], w_in[:, 2048:3072]
    q_w, k_w, v_w = w_in[:, 3072:4096], w_in[:, 4096:4352], w_in[:, 4352:4608]
    ga_w, gb_w = w_in[:, 4608:5632], w_in[:, 5632:6656]
    perm = _qperm()
    wa = f(inp["w_a_out"])[0]
    wb = f(inp["w_b_out"])[0][perm, :]
    w_up = f(inp["w_up"])[0]
    cx, cc, cb = _chunks(xin_w), _chunks(c_w), _chunks(b_w)
    ca, cbm, cga, cgb = _chunks(wa), _chunks(wb), _chunks(ga_w), _chunks(gb_w)
    cu = _chunks(w_up)
    parts = [_chunks(f(inp["w_ada"])[0]), _chunks(k_w), _chunks(q_w[:, perm])]
    for j in range(8):
        parts += [cx[j:j + 1], cc[j:j + 1], cb[j:j + 1]]
    for j in range(8):
        parts += [ca[j:j + 1], cbm[j:j + 1], cga[j:j + 1], cgb[j:j + 1]]
    for p in range(22):
        parts += [cu[p:p + 1], cu[22 + p:23 + p]]
    parts.append(np.zeros((2, 128, 8, 128), np.float32))
    WS = np.ascontiguousarray(np.concatenate(parts, 0))
    assert WS.shape[0] == 160

    def fm(v, n):
        return np.asarray(v, np.float32).reshape(n, 128).T

    pf = np.zeros((128, NPF), np.float32)
    pf[:, PF_GM:PF_GM + 8] = fm(inp["g_mix"][0], 8)
    caw = f(inp["conv_a_w"])[0]
    for i in range(3):
        pf[:, PF_CW + i * 8:PF_CW + i * 8 + 8] = fm(caw[i], 8)
    pf[:, PF_GF:PF_GF + 8] = fm(inp["g_ffn"][0], 8)
    fcw = f(inp["ffn_conv_w"])[0]
    for i in range(3):
        pf[:, PF_FW + i * 44:PF_FW + i * 44 + 44] = fm(fcw[i], 44)
    pf[:, PF_FB:PF_FB + 44] = fm(inp["ffn_conv_b"][0], 44)
    pf[:, PF_BA:PF_BA + 48] = fm(inp["b_ada"][0], 48)
    pr = np.zeros((1, 1040), np.float32)
    pr[0, 0:1024] = inp["g_final"]
    pr[0, 1024:1040] = inp["attn_sinks"][0]
    cst = np.zeros((128, 384), np.float32)
    cst[:, 0:128] = np.eye(128, dtype=np.float32)
    s = np.arange(128)[:, None]
    q = np.arange(128)[None, :]
    cst[:, 128:256] = (s >= q)
    cst[:, 256:384] = (s <= q)
    xp = f(inp["x_prompt"])[0]
    xs = f(inp["x_sample"])[:, 0]
    cp = f(inp["c_prompt"])[0]
    cs = f(inp["c_sample"])
    shared = dict(WS=WS, wv=f(v_w), wk=f(k_w), wo=f(inp["w_o"])[0], wd=f(inp["w_down"])[0], pr=pr, cst=cst)
    maps = []
    for i in range(NCORES):
        X = np.zeros((2352, D), np.float32)
        if i > 0:
            X[0:256] = xp[2048 * i - 256:2048 * i]
        X[C_SMP:C_SMP + 16] = xs[16 * i:16 * i + 16]
        X[C_OWN:C_OWN + 2048] = xp[2048 * i:2048 * i + 2048]
        ct = np.zeros((48, D), np.float32)
        ct[0] = cp
        ct[1] = cp
        ct[32:48] = cs[16 * i:16 * i + 16]
        cT = np.ascontiguousarray(ct.reshape(48, 8, 128).transpose(2, 1, 0)).reshape(128, 8 * 48)
        pfi = pf.copy()
        pfi[:, PF_FL] = 0.0 if i == 0 else 1.0
        m = dict(shared)
        m.update(x=X, cT=cT, pf=pfi,
                 ck=f(inp["cache_k_win"])[0, 16 * i:16 * i + 16].reshape(16, 128, 256),
                 cv=f(inp["cache_v_win"])[0, 16 * i:16 * i + 16].reshape(16, 128, 256),
                 sca=f(inp["state_conv_a"])[0, 16 * i:16 * i + 16],
                 sff=f(inp["state_ffn_conv"])[0, 16 * i:16 * i + 16])
        maps.append(m)
    return maps


def _assemble(results):
    r = results
    y_prompt = np.concatenate([r[i]["y"] for i in range(NCORES)], 0)[None]
    y_sample = np.concatenate([r[i]["ys"] for i in range(NCORES)], 0)[:, None, :]
    nca_p = r[NCORES - 1]["ncap"][None, None]
    nca_s = np.concatenate([r[i]["ncas"] for i in range(NCORES)], 0)[None]
    nk_p = r[NCORES - 1]["nkp"].reshape(1, 1, 128, 4, 64)
    nk_s = np.concatenate([r[i]["nks"] for i in range(NCORES)], 0).reshape(1, 128, 128, 4, 64)
    nv_p = r[NCORES - 1]["nvp"].reshape(1, 1, 128, 4, 64)
    nv_s = np.concatenate([r[i]["nvs"] for i in range(NCORES)], 0).reshape(1, 128, 128, 4, 64)
    nf_p = r[NCORES - 1]["nfp"][None, None]
    nf_s = np.concatenate([r[i]["nfs"] for i in range(NCORES)], 0)[None]
    outs = (y_prompt, y_sample, nca_p, nca_s, nk_p, nk_s, nv_p, nv_s, nf_p, nf_s)
    return tuple(np.ascontiguousarray(o, dtype=np.float32) for o in outs)


def kernel(**inputs):
    maps = _host_prep(inputs)
    bld = Builder()
    nc = bld.build()
    res = run_bass_kernel_spmd(nc, maps, core_ids=list(range(NCORES)))
    return _assemble(res.results)
```

```python
import numpy as np
import concourse.bass as bass
import concourse.mybir as mybir
from concourse.bass_utils import run_bass_kernel_spmd

F32 = mybir.dt.float32
BF16 = mybir.dt.bfloat16
AF = mybir.ActivationFunctionType
ALU = mybir.AluOpType

ENGS = ("pe", "act", "dve", "pool", "sp")
NCORES = 8
D = 1024
NCOL = 1328
C_SB = 254
C_SMP = 286
C_OWN = 304
MC0 = 252
MW = NCOL - MC0
NSLOT = 4
SEGB = [0, 128, 256, 304] + [304 + 128 * i for i in range(1, 9)]
PF_GM, PF_CW, PF_GF, PF_FW, PF_FB, PF_BA, PF_FL, NPF = 0, 8, 32, 40, 172, 216, 264, 265
EPS = 1e-6


def segs(c0, c1):
    return [i for i in range(len(SEGB) - 1) if SEGB[i] < c1 and SEGB[i + 1] > c0]


class Op:
    __slots__ = ("eng", "emit", "idx", "deps", "flag", "ticket", "dma_sem", "dma_cnt", "waits")

    def __init__(self, eng, emit):
        self.eng = eng
        self.emit = emit
        self.deps = []
        self.flag = False
        self.ticket = 0
        self.dma_sem = None
        self.dma_cnt = 0
        self.waits = []


class Prog:
    def __init__(self, nc):
        self.nc = nc
        self.streams = {e: [] for e in ENGS}
        self.last_writer = {}
        self.readers = {}
        self.dma_counts = {}
        self.final_dma = []
        self.pending_dma = []
        self.bar_deps = {}
        self.open_group = {}
        self.last_fin = None
        self.alias = {}
        self._bank = 0

    def bank(self):
        b = self._bank
        self._bank = (b + 1) % 8
        return b

    def add_alias(self, name, w0, w1):
        self.alias[name] = True

    def phase_switch(self, emit):
        return self._add(Op("dve", emit), (), ("scrtok",))

    def _add(self, op, reads, writes):
        if any((n in self.alias) for n in reads) or any((n in self.alias) for n in writes):
            reads = tuple(reads) + ("scrtok",)
        deps = set()
        for r in reads:
            lw = self.last_writer.get(r)
            if lw is not None:
                deps.add(lw)
        for w in writes:
            lw = self.last_writer.get(w)
            if lw is not None:
                deps.add(lw)
            for rd in self.readers.get(w, ()):
                deps.add(rd)
        bd = self.bar_deps.pop(op.eng, None)
        if bd:
            deps.update(bd)
        deps.discard(op)
        op.deps = list(deps)
        for r in reads:
            self.readers.setdefault(r, []).append(op)
        for w in writes:
            self.last_writer[w] = op
            self.readers[w] = []
        op.idx = len(self.streams[op.eng])
        self.streams[op.eng].append(op)
        return op

    def op(self, eng, emit, reads=(), writes=()):
        return self._add(Op(eng, emit), tuple(reads), tuple(writes))

    def dma(self, eng, sem_key, emit, reads=(), writes=(), final=False):
        op = Op(eng, emit)
        c = self.dma_counts.get(sem_key, 0) + 1
        self.dma_counts[sem_key] = c
        op.dma_sem = sem_key
        op.dma_cnt = 16 * c
        self._add(op, tuple(reads), tuple(writes))
        if sem_key == "fin":
            prev = self.last_fin
            if prev is not None and prev not in op.deps:
                op.deps.append(prev)
            self.last_fin = op
        self.open_group.setdefault(sem_key, []).append(op)
        self.pending_dma.append(op)
        if final:
            self.final_dma.append(op)
        return op

    def end_group(self, key):
        tot = 16 * self.dma_counts.get(key, 0)
        for op in self.open_group.get(key, []):
            op.dma_cnt = tot
        self.open_group[key] = []

    def barrier(self):
        lasts = [self.streams[e][-1] for e in ENGS if self.streams[e]]
        lasts += self.pending_dma
        self.pending_dma = []
        self.bar_deps = {e: list(lasts) for e in ENGS}

    def finalize_and_emit(self):
        nc = self.nc
        for e in ENGS:
            for op in self.streams[e]:
                for d in op.deps:
                    if d.dma_sem is None:
                        d.flag = True
        for e in ENGS:
            t = 0
            for op in self.streams[e]:
                if op.dma_sem is None and op.flag:
                    t += 1
                    op.ticket = t
        sem_names = ["eng_" + e for e in ENGS] + ["dma_" + str(k) for k in self.dma_counts]
        sems = {n: nc.alloc_semaphore(name=n) for n in sem_names}
        nwaits = 0
        for e in ENGS:
            waited = {}
            for op in self.streams[e]:
                need = {}
                for d in op.deps:
                    if d.dma_sem is not None:
                        k, v = "dma_" + str(d.dma_sem), d.dma_cnt
                    else:
                        k, v = "eng_" + d.eng, d.ticket
                    if need.get(k, 0) < v:
                        need[k] = v
                op.waits = []
                for k, v in need.items():
                    if waited.get(k, 0) < v:
                        waited[k] = v
                        op.waits.append((k, v))
                        nwaits += 1
        final_waits = {}
        for op in self.final_dma:
            k = "dma_" + str(op.dma_sem)
            final_waits[k] = max(final_waits.get(k, 0), op.dma_cnt)
        engmap = {"pe": "tensor", "act": "scalar", "dve": "vector", "pool": "gpsimd", "sp": "sync"}
        with nc.Block() as block:
            for e in ENGS:
                ops = self.streams[e]

                def body(engine, ops=ops, e=e):
                    for op in ops:
                        for k, v in op.waits:
                            engine.wait_ge(sems[k], v)
                        ins = op.emit(engine)
                        if op.dma_sem is not None:
                            ins.then_inc(sems["dma_" + str(op.dma_sem)], 16)
                        elif op.flag:
                            ins.then_inc(sems["eng_" + e], 1)
                    if e == "sp":
                        for k, v in final_waits.items():
                            engine.wait_ge(sems[k], v)

                getattr(block, engmap[e])(body)
        stats = {e: len(self.streams[e]) for e in ENGS}
        stats["sems"] = len(sem_names)
        stats["waits"] = nwaits
        return stats


class Builder:
    def __init__(self, debug=False, halves=None):
        import os
        self.debug = debug
        if halves is None:
            halves = tuple(int(x) for x in os.environ.get("K_HALVES", "0,1").split(","))
        self.halves = halves
        nc = self.nc = bass.Bass("TRN2", target_bir_lowering=False)
        self.P = Prog(nc)
        self.taps = []
        self._decl()

    def din(self, name, shape):
        return self.nc.dram_tensor(name, list(shape), F32, kind="ExternalInput").ap()

    def dout(self, name, shape):
        return self.nc.dram_tensor(name, list(shape), F32, kind="ExternalOutput").ap()

    def sb(self, name, shape, dt):
        return self.nc.alloc_sbuf_tensor(name, list(shape), dt)

    def _decl(self):
        nc = self.nc
        self.x_d = self.din("x", [2352, D])
        self.cT_d = self.din("cT", [128, 8 * 48])
        self.pf_d = self.din("pf", [128, NPF])
        self.pr_d = self.din("pr", [1, 1040])
        self.cst_d = self.din("cst", [128, 384])
        self.WS_d = self.din("WS", [160, 128, 8, 128])
        self.wv_d = self.din("wv", [D, 256])
        self.wk_d = self.din("wk", [D, 256])
        self.wo_d = self.din("wo", [D, D])
        self.wd_d = self.din("wd", [2816, D])
        self.ck_d = self.din("ck", [16, 128, 256])
        self.cv_d = self.din("cv", [16, 128, 256])
        self.sca_d = self.din("sca", [16, 2, D])
        self.sff_d = self.din("sff", [16, 2, 5632])
        self.y_d = self.dout("y", [2048, D])
        self.ys_d = self.dout("ys", [16, D])
        self.ncap_d = self.dout("ncap", [2, D])
        self.ncas_d = self.dout("ncas", [16, 2, D])
        self.nkp_d = self.dout("nkp", [128, 256])
        self.nks_d = self.dout("nks", [16, 128, 256])
        self.nvp_d = self.dout("nvp", [128, 256])
        self.nvs_d = self.dout("nvs", [16, 128, 256])
        self.nfp_d = self.dout("nfp", [2, 5632])
        self.nfs_d = self.dout("nfs", [16, 2, 5632])
        self.x1_d = nc.dram_tensor("x1scr", [2048, D], F32, kind="Internal").ap()
        self.vs_d = nc.dram_tensor("vsscr", [16, 256], F32, kind="Internal").ap()

        NE = 8 * NCOL + 2 * NCOL + 10 * 256 + 8 * 1024
        self.EREG = self.sb("EREG", [128, NE], BF16)
        o = 0
        self.HT = self.EREG[:, o:o + 8 * NCOL].rearrange("p (k n) -> p k n", k=8); o += 8 * NCOL
        self.KT = self.EREG[:, o:o + 2 * NCOL].rearrange("p (k n) -> p k n", k=2); o += 2 * NCOL
        self.VT = self.EREG[:, o:o + 2560].rearrange("p (k n) -> p k n", k=10); o += 2560
        self.WO = self.EREG[:, o:o + 8192].rearrange("p (k n) -> p k n", k=8); o += 8192
        self.WD = self.EREG[:, 0:22 * 1024].rearrange("p (k n) -> p k n", k=22)
        self.MREGF = self.sb("MREG", [128, 24 * MW], BF16)
        self.MREG = self.MREGF[:, :].rearrange("p (k n) -> p k n", k=24)
        self.WV = self.MREGF[:, 22 * MW:22 * MW + 2048].rearrange("p (k n) -> p k n", k=8)
        self.KCT = self.MREGF[:, 16 * MW:16 * MW + 4096].rearrange("p (c b s) -> p c b s", c=2, b=16)
        self.VC = self.MREGF[:, 16 * MW + 4096:16 * MW + 8192].rearrange("p (b f) -> p b f", b=16)
        self.WR = self.sb("WR", [128, NSLOT, 4, 8, 128], BF16)
        self.ADAT = self.sb("ADAT", [128, 48, 48], F32)
        self.GT1BC = self.sb("GT1BC", [128, D], F32)
        self.GT2BC = self.sb("GT2BC", [128, D], F32)
        self.GT1SB = self.sb("GT1SB", [48, D], F32)
        self.GT2SB = self.sb("GT2SB", [48, D], F32)
        self.GFIN = self.sb("GFIN", [128, D], F32)
        self.XR = self.sb("XR", [128, 2, D], F32)
        self.XN = self.sb("XN", [128, 2, D], BF16)
        self.JK = self.sb("JK", [128, D], BF16)
        self.CST = self.sb("CST", [128, 384], F32)
        self.IDB = self.sb("IDB", [128, 128], BF16)
        self.MPREV = self.sb("MPREV", [128, 4, 128], BF16)
        self.MCUR = self.sb("MCUR", [128, 4, 128], BF16)
        self.MPREV0 = self.sb("MPREV0", [128, 4, 128], BF16)
        self.SINKP = self.sb("SINKP", [128, 2, 4], F32)
        self.ONESB = self.sb("ONESB", [128, 128], BF16)
        self.PF = self.sb("PF", [128, NPF], F32)
        self.SINKB = self.sb("SINKB", [128, 16], F32)
        self.SINKE = self.sb("SINKE", [128, 16], F32)
        self.EPSC = self.sb("EPSC", [128, 1], F32)
        self.TOK = self.sb("TOK", [128, 1], F32)
        self.KCAR = self.sb("KCAR", [128, 2, 128], BF16)
        self.VCAR = self.sb("VCAR", [128, 256], BF16)
        self.SS = self.sb("SS", [128, 64], F32)
        self.RS = self.sb("RS", [128, 64], F32)
        self.ULAST = self.sb("ULAST", [128, 8, 2], F32)
        self.UPLAST = self.sb("UPLAST", [128, 44, 2], F32)
        self.SCR = self.sb("SCR", [128, 6784], F32)
        self.CT = self.SCR[:, 0:384].rearrange("p (k t) -> p k t", k=8)
        self.SCT = self.sb("SCT", [128, 8, 48], BF16)
        self.USM = self.SCR[:, 5268:5668].rearrange("p (k t) -> p k t", k=8)
        self.BSM = self.SCR[:, 5668:6052].rearrange("p (k t) -> p k t", k=8)
        self.CSM = self.SCR[:, 6052:6436].rearrange("p (k t) -> p k t", k=8)
        self.STA = self.SCR[:, 6436:6692].rearrange("p (k t) -> p k t", k=8)
        self.PS = [nc.alloc_psum_tensor("ps%d" % i, [128, 512], F32) for i in range(8)]
        self.ss_col = 0

    def psb(self, b):
        return self.PS[b][:, :].bitcast(BF16)

    def act(self, out, in_, func, reads, writes, bias=None, scale=None, accum_out=None):
        kw = {}
        if bias is not None:
            kw["bias"] = bias
        if scale is not None:
            kw["scale"] = scale
        if accum_out is not None:
            kw["accum_out"] = accum_out
        return self.P.op("act", lambda e: e.activation(out=out, in_=in_, func=func, **kw), reads, writes)

    def tt(self, eng, out, in0, in1, op, reads, writes):
        return self.P.op(eng, lambda e: e.tensor_tensor(out=out, in0=in0, in1=in1, op=op), reads, writes)

    def ts(self, eng, out, in0, s1, s2, op0, op1, reads, writes):
        if op1 is None:
            return self.P.op(eng, lambda e: e.tensor_scalar(out=out, in0=in0, scalar1=s1, scalar2=None, op0=op0), reads, writes)
        return self.P.op(eng, lambda e: e.tensor_scalar(out=out, in0=in0, scalar1=s1, scalar2=s2, op0=op0, op1=op1), reads, writes)

    def stt(self, eng, out, in0, scalar, in1, op0, op1, reads, writes):
        return self.P.op(eng, lambda e: e.scalar_tensor_tensor(out=out, in0=in0, scalar=scalar, in1=in1, op0=op0, op1=op1), reads, writes)

    def copy(self, eng, out, in_, reads, writes):
        if eng == "act":
            return self.act(out, in_, AF.Copy, reads, writes)
        return self.P.op(eng, lambda e: e.tensor_copy(out=out, in_=in_), reads, writes)

    def mm(self, seq, reads, writes, start=True, stop=True):
        n = len(seq)

        def emit(e):
            ins = None
            for i, (o, l, r) in enumerate(seq):
                ins = e.matmul(o, lhsT=l, rhs=r, start=(start and i == 0), stop=(stop and i == n - 1))
            return ins
        return self.P.op("pe", emit, reads, writes)

    def mm_multi(self, groups, reads, writes):
        def emit(e):
            ins = None
            for seq in groups:
                n = len(seq)
                for i, (o, l, r) in enumerate(seq):
                    ins = e.matmul(o, lhsT=l, rhs=r, start=(i == 0), stop=(i == n - 1))
            return ins
        return self.P.op("pe", emit, reads, writes)

    def tr(self, items, reads, writes):
        def emit(e):
            ins = None
            for (o, i_, idn) in items:
                ins = e.transpose(out=o, in_=i_, identity=idn)
            return ins
        return self.P.op("pe", emit, reads, writes)

    def ld(self, key, out, in_, reads=(), writes=(), eng="sp"):
        return self.P.dma(eng, key, lambda e: e.dma_start(out=out, in_=in_), reads, writes)

    def st(self, key, out, in_, reads=(), writes=(), final=True, eng="sp"):
        return self.P.dma(eng, key, lambda e: e.dma_start(out=out, in_=in_), reads, writes, final=final)

    def tap(self, name, ap, shape, reads):
        if not self.debug:
            return
        d = self.nc.dram_tensor("dbg_" + name, list(shape), ap.dtype, kind="ExternalOutput").ap()
        self.taps.append(name)
        self.st("dbg_" + name, d, ap, reads=reads)

    def w_init(self):
        seq = list(range(0, 16)) + list(range(48, 58)) + list(range(16, 48)) + list(range(58, 158))
        if 1 in self.halves:
            seq += list(range(48, 158))
        self.wseq = seq
        groups = []
        self.wpos = []
        for pos, d in enumerate(seq):
            if groups and groups[-1][1] < 4 and groups[-1][0] + groups[-1][1] == d:
                g = groups[-1]
                groups[-1] = (g[0], g[1] + 1, g[2])
            else:
                groups.append((d, 1, pos))
            self.wpos.append((len(groups) - 1, pos - groups[-1][2]))
        self.wgroups = groups
        self.w_next = 0

    def wp(self, h, off):
        if h == 0:
            return 16 + off if off < 10 else 48 + off
        return 158 + off

    def w_use(self, pos):
        g, sub = self.wpos[pos]
        while self.w_next < len(self.wgroups) and self.w_next < g + NSLOT - 1:
            gl = self.w_next
            ds, n, _ = self.wgroups[gl]
            slot = gl % NSLOT
            self.P.dma("pool", "w%d" % slot,
                       lambda e, slot=slot, ds=ds, n=n: e.dma_start(
                           out=self.WR[:, slot, 0:n], in_=self.WS_d[ds:ds + n].rearrange("c p k n -> p c k n")),
                       writes=[("w", slot)])
            self.w_next += 1
        return self.WR[:, g % NSLOT, sub], ("w", g % NSLOT)

    def p0_consts(self):
        P = self.P
        self.ld("p0", self.CST[:], self.cst_d[:, :], writes=["CST"])
        self.ld("p0", self.PF[:], self.pf_d[:, :], writes=["PF"])
        self.ld("p0", self.SCR[:, 0:384], self.cT_d[:, :], writes=["CT"])
        self.ld("p0", self.GFIN[:], self.pr_d[0:1, 0:1024].broadcast_to([128, 1024]), writes=["GFIN"])
        self.ld("p0", self.SINKB[:], self.pr_d[0:1, 1024:1040].broadcast_to([128, 16]), writes=["SINKB"])
        self.P.end_group("p0")
        self.copy("dve", self.IDB[:], self.CST[:, 0:128], ["CST"], ["IDB"])
        mp = self.CST[:, 128:256].unsqueeze(1).broadcast_to([128, 4, 128])
        mc = self.CST[:, 256:384].unsqueeze(1).broadcast_to([128, 4, 128])
        self.ts("dve", self.MPREV[:], mp, -1.0, 30000.0, ALU.add, ALU.mult, ["CST"], ["masks"])
        self.ts("dve", self.MCUR[:], mc, -1.0, 30000.0, ALU.add, ALU.mult, ["CST"], ["masks"])
        TMPM = self.SCR[:, 1024:1536].rearrange("p (g q) -> p g q", g=4)
        self.ts("dve", TMPM, mp, self.PF[:, PF_FL:PF_FL + 1], -1.0, ALU.mult, ALU.add, ["CST", "PF"], ["tmpm"])
        self.ts("dve", self.MPREV0[:], TMPM, 30000.0, None, ALU.mult, None, ["tmpm"], ["masks"])
        P.op("dve", lambda e: e.memset(self.ONESB[:], 1.0), [], ["ONESB"])
        P.op("dve", lambda e: e.memset(self.EPSC[:], EPS), [], ["EPSC"])
        self.act(self.SINKE[:], self.SINKB[:], AF.Exp, ["SINKB"], ["SINKE"])
        for c in range(2):
            for par in range(2):
                kv = 2 * c + par
                self.copy("dve", self.SINKP[par * 64:par * 64 + 64, c, :], self.SINKE[par * 64:par * 64 + 64, kv * 4:kv * 4 + 4],
                          ["SINKE"], ["SINKP"])
        self.act(self.SCT[:], self.CT, AF.Silu, ["CT"], ["SCT"])

    def cache_copies(self):
        self.st("fin", self.nks_d[:, 0:127, :], self.ck_d[:, 1:128, :])
        self.st("fin", self.nvs_d[:, 0:127, :], self.cv_d[:, 1:128, :])
        self.st("fin", self.ncas_d[:, 0, :], self.sca_d[:, 1, :])
        self.st("fin", self.nfs_d[:, 0, :], self.sff_d[:, 1, :])

    def p1_ada(self, part, ms=None, finish=True, bank=None):
        P = self.P
        if ms is None:
            ms = range(0, 16) if part == 0 else range(16, 48)
        for m in ms:
            w, wr = self.w_use(m if m < 16 else 26 + (m - 16))
            b = P.bank() if bank is None else bank
            self.mm([(self.PS[b][:, 0:48], w[:, k, :], self.SCT[:, k, :]) for k in range(8)],
                    [wr, "SCT"], [("ps", b)])
            self.act(self.ADAT[:, m, :], self.PS[b][:, 0:48], AF.Identity, [("ps", b), "PF"], [("ada", m)],
                     bias=self.PF[:, PF_BA + m:PF_BA + m + 1])
        if not finish:
            return
        if part == 0:
            gm = self.PF[:, PF_GM:PF_GM + 8].unsqueeze(2).broadcast_to([128, 8, 48])
            self.stt("dve", self.ADAT[:, 8:16, :], self.ADAT[:, 8:16, :], 1.0, gm, ALU.add, ALU.mult,
                     [("ada", m) for m in range(8, 16)] + ["PF"], [("ada", m) for m in range(8, 16)])
            return
        gf = self.PF[:, PF_GF:PF_GF + 8].unsqueeze(2).broadcast_to([128, 8, 48])
        self.stt("dve", self.ADAT[:, 32:40, :], self.ADAT[:, 32:40, :], 1.0, gf, ALU.add, ALU.mult,
                 [("ada", m) for m in range(32, 40)] + ["PF"], [("ada", m) for m in range(32, 40)])
        idf = self.CST[:, 0:128]
        for (base, BC, SBt, nm) in ((16, self.GT1BC, self.GT1SB, "GT1"), (40, self.GT2BC, self.GT2SB, "GT2")):
            for half in range(2):
                b = P.bank()
                self.tr([(self.PS[b][:, j * 128:(j + 1) * 128],
                          self.ADAT[:, base + half * 4 + j, 0:1].broadcast_to([128, 128]), idf) for j in range(4)],
                        [("ada", base + half * 4 + j) for j in range(4)] + ["CST"], [("ps", b)])
                self.copy("dve", BC[:, half * 512:(half + 1) * 512], self.PS[b][:, :], [("ps", b)], [nm + "BC"])
                b = P.bank()
                self.tr([(self.PS[b][0:48, j * 128:(j + 1) * 128], self.ADAT[:, base + half * 4 + j, :], idf)
                         for j in range(4)],
                        [("ada", base + half * 4 + j) for j in range(4)] + ["CST"], [("ps", b)])
                self.copy("dve", SBt[:, half * 512:(half + 1) * 512], self.PS[b][0:48, :], [("ps", b)], [nm + "SB"])

    def rstd(self, n, ssc):
        self.act(self.RS[0:n, ssc:ssc + 1], self.SS[0:n, ssc:ssc + 1], AF.Sqrt, [("ss", ssc), "EPSC"], [("rs", ssc)],
                 bias=self.EPSC[0:n, 0:1], scale=1.0 / D)
        self.P.op("dve", lambda e: e.reciprocal(out=self.RS[0:n, ssc:ssc + 1], in_=self.RS[0:n, ssc:ssc + 1]),
                  [("rs", ssc)], [("rs", ssc)])

    def norm1a(self, n, src, src_res, ssc):
        self.act(self.JK[0:n, :], src, AF.Square, src_res, ["JK", ("ss", ssc)], accum_out=self.SS[0:n, ssc:ssc + 1])
        self.act(self.RS[0:n, ssc:ssc + 1], self.SS[0:n, ssc:ssc + 1], AF.Sqrt, [("ss", ssc), "EPSC"], [("rs", ssc)],
                 bias=self.EPSC[0:n, 0:1], scale=1.0 / D)

    def norm1b(self, n, src, src_res, ssc, xslot):
        self.P.op("dve", lambda e: e.reciprocal(out=self.RS[0:n, ssc:ssc + 1], in_=self.RS[0:n, ssc:ssc + 1]),
                  [("rs", ssc)], [("rs", ssc)])
        self.ts("dve", self.XN[0:n, xslot, :], src, self.RS[0:n, ssc:ssc + 1], None, ALU.mult, None,
                src_res + [("rs", ssc)], [("xn", xslot)])

    def norm1(self, n, src, src_res, ssc, xslot):
        self.norm1a(n, src, src_res, ssc)
        self.norm1b(n, src, src_res, ssc, xslot)

    def norm2(self, n, xslot, col0, sh_base, g_base, with_samples, ndve=4):
        P = self.P
        na = 8 - ndve
        ba = P.bank()
        bb = P.bank()
        pva = self.psb(ba)
        pvb = self.psb(bb)
        pj = lambda j: (pva[:, j * 128:j * 128 + n] if j < na else pvb[:, (j - na) * 128:(j - na) * 128 + n])
        self.tr([(pj(j), self.XN[0:n, xslot, j * 128:(j + 1) * 128], self.IDB[0:n, 0:n]) for j in range(na)],
                [("xn", xslot), "IDB"], [("ps", ba)])
        self.tr([(pj(j), self.XN[0:n, xslot, j * 128:(j + 1) * 128], self.IDB[0:n, 0:n]) for j in range(na, 8)],
                [("xn", xslot), "IDB"], [("ps", bb)])
        sg = segs(col0, col0 + n)
        for j in range(8):
            if j < na:
                self.act(self.HT[:, j, col0:col0 + n], pj(j), AF.Identity,
                         [("ps", ba), ("ada", sh_base + j), ("ada", g_base + j), "WDdone"], [("h", j, s) for s in sg],
                         bias=self.ADAT[:, sh_base + j, 0:1], scale=self.ADAT[:, g_base + j, 0:1])
            else:
                self.ts("dve", self.HT[:, j, col0:col0 + n], pj(j), self.ADAT[:, g_base + j, 0:1], self.ADAT[:, sh_base + j, 0:1],
                        ALU.mult, ALU.add, [("ps", bb), ("ada", sh_base + j), ("ada", g_base + j), "WDdone"], [("h", j, s) for s in sg])
        if with_samples:
            o0 = C_SMP - col0
            tmp = self.SCR[:, 6656:6784].rearrange("p (j t) -> p j t", j=8)
            pa3 = pva[:, 0:na * 128].rearrange("p (j t) -> p j t", j=na)[:, :, o0:o0 + 16]
            pb3 = pvb[:, 0:ndve * 128].rearrange("p (j t) -> p j t", j=ndve)[:, :, o0:o0 + 16]
            self.copy("act", tmp[:, 0:na, :], pa3, [("ps", ba)], ["scr_smp"])
            self.copy("dve", tmp[:, na:8, :], pb3, [("ps", bb)], ["scr_smp"])
            self.tt("dve", tmp, tmp, self.ADAT[:, g_base:g_base + 8, 32:48], ALU.mult,
                    ["scr_smp"] + [("ada", g_base + j) for j in range(8)], ["scr_smp"])
            self.tt("dve", self.HT[:, :, C_SMP:C_SMP + 16], tmp, self.ADAT[:, sh_base:sh_base + 8, 32:48], ALU.add,
                    ["scr_smp"] + [("ada", sh_base + j) for j in range(8)], [("h", j, 2) for j in range(8)])

    def norm_transpose(self, n, src, src_res, ssc, xslot, col0, sh_base, g_base, with_samples):
        self.norm1(n, src, src_res, ssc, xslot)
        self.norm2(n, xslot, col0, sh_base, g_base, with_samples)

    def p2_h1(self, h):
        blocks = []
        if h == 0:
            blocks += [(0, 128, 0), (128, 128, 128), (256, 48, 256)]
            blocks += [(C_OWN + 128 * b, 128, C_OWN + 128 * b) for b in range(8)]
        else:
            blocks += [(C_OWN + 128 * b, 128, NCOL + 128 * b) for b in range(8)]
        q1 = q2 = None
        for i, (col0, n, row0) in enumerate(blocks):
            xs = i % 2
            self.ld("xr%d" % xs, self.XR[0:n, xs, :], self.x_d[row0:row0 + n, :], writes=[("xr", xs)])
            ssc = self.ss_col
            self.ss_col = (self.ss_col + 1) % 64
            if q2 is not None:
                self.norm2(*q2)
                q2 = None
            if q1 is not None:
                self.norm1b(*q1[0])
                q2 = q1[1]
            self.norm1a(n, self.XR[0:n, xs, :], [("xr", xs)], ssc)
            q1 = ((n, self.XR[0:n, xs, :], [("xr", xs)], ssc, xs), (n, xs, col0, 0, 8, (h == 0 and col0 == 256)))
        if q2 is not None:
            self.norm2(*q2)
        self.norm1b(*q1[0])
        self.norm2(*q1[1])

    def ntiles(self, h, small0):
        t = []
        if h == 0:
            t.append((small0, 302 - small0))
        t += [(C_OWN, 512), (C_OWN + 512, 512)]
        return t

    def p3_qkv(self, h):
        P = self.P
        HT, KT = self.HT, self.KT
        mres_wv = [("M", c, s) for c in (22, 23) for s in range(11)]
        self.P.dma("pool", "wvk", lambda e: e.dma_start(out=self.WV, in_=self.wv_d.rearrange("(k p) n -> p k n", p=128)),
                   writes=mres_wv)
        WK2 = self.SCR[:, 0:1024].bitcast(BF16).rearrange("p (k n) -> p k n", k=8)
        self.P.dma("pool", "wvk", lambda e: e.dma_start(out=WK2, in_=self.wk_d.rearrange("(k p) n -> p k n", p=128)),
                   writes=["WK2"])
        self.P.end_group("wvk")
        import os
        skip = os.environ.get("K_SKIP", "").split(",")
        if h == 1 and "carry" not in skip:
            self.copy("dve", KT[:, :, 128:256], self.KCAR[:, :, :], ["KCAR", "WDdone"], [("k", c, 1) for c in range(2)])
            self.copy("dve", self.VT[:, 1, :], self.VCAR[:, :], ["VCAR", "WDdone"], [("v", 1)])
        ktiles = ([(0, 304)] if h == 0 else []) + [(C_OWN, 512), (C_OWN + 512, 512)]
        for c in range(2):
            w, wr = self.w_use(self.wp(h, c))
            for (c0, n) in ktiles:
                b = P.bank()
                sg = segs(c0, c0 + n)
                self.mm([(self.PS[b][:, 0:n], w[:, k, :], HT[:, k, c0:c0 + n]) for k in range(8)],
                        [wr] + [("h", k, s) for k in range(8) for s in sg], [("ps", b)])
                self.copy("act", KT[:, c, c0:c0 + n], self.PS[b][:, 0:n], [("ps", b), "WDdone"], [("k", c, s) for s in sg])
        for qc in range(8):
            w, wr = self.w_use(self.wp(h, 2 + qc))
            for (c0, n) in self.ntiles(h, C_SB):
                b = P.bank()
                sg = segs(c0, c0 + n)
                self.mm([(self.PS[b][:, 0:n], w[:, k, :], HT[:, k, c0:c0 + n]) for k in range(8)],
                        [wr] + [("h", k, s) for k in range(8) for s in sg], [("ps", b)])
                self.act(self.MREG[:, qc, c0 - MC0:c0 - MC0 + n], self.PS[b][:, 0:n], AF.Copy, [("ps", b)],
                         [("M", qc, s) for s in sg], scale=0.125)
        vblocks = ([(0, 0), (1, 128)] if h == 0 else []) + [(2 + lb, C_OWN + 128 * lb) for lb in range(8)]
        for (vb, c0) in vblocks:
            b = P.bank()
            sg = segs(c0, c0 + 128)
            self.mm([(self.PS[b][:, 0:256], HT[:, k, c0:c0 + 128], self.WV[:, k, :]) for k in range(8)],
                    mres_wv + [("h", k, s) for k in range(8) for s in sg], [("ps", b)])
            self.copy("act", self.VT[:, vb, :], self.PS[b][:, 0:256], [("ps", b), "WDdone"], [("v", vb)])
            if h == 1 and vb == 9 and "last" not in skip:
                if "lastv" not in skip:
                    VL = self.SCR[:, 1024:1280]
                    self.copy("act", VL, self.PS[b][:, 0:256], [("ps", b)], ["VL"])
                    if "nostore" not in skip:
                        src = self.VT[:, 9, :] if "altsrc" in skip else VL
                        dst = self.nkp_d if "altdst" in skip else self.nvp_d
                        self.st("fin", dst[:, :], VL, reads=["VL"])
                if "lastk" not in skip:
                    b2 = P.bank()
                    self.mm([(self.PS[b2][:, 0:256], HT[:, k, c0:c0 + 128], WK2[:, k, :]) for k in range(8)],
                            ["WK2"] + [("h", k, s) for k in range(8) for s in sg], [("ps", b2)])
                    KL = self.SCR[:, 1280:1536]
                    self.copy("dve", KL, self.PS[b2][:, 0:256], [("ps", b2)], ["KL"])
                    self.st("fin", self.nkp_d[:, :], KL, reads=["KL"])
        if h == 0:
            self.copy("dve", self.KCAR[:, :, :], KT[:, :, NCOL - 128:NCOL], [("k", c, 10) for c in range(2)], ["KCAR"])
            self.copy("dve", self.VCAR[:, :], self.VT[:, 9, :], [("v", 9)], ["VCAR"])
            b = P.bank()
            self.mm_multi([[(self.PS[b][0:16, 0:256], HT[:, k, C_SMP:C_SMP + 16], self.WV[:, k, :]) for k in range(8)],
                           [(self.PS[b][0:16, 256:512], HT[:, k, C_SMP:C_SMP + 16], WK2[:, k, :]) for k in range(8)]],
                          mres_wv + ["WK2"] + [("h", k, 2) for k in range(8)], [("ps", b)])
            VKS = self.SCR[0:16, 1536:2048]
            self.copy("dve", VKS, self.PS[b][0:16, :], [("ps", b)], ["VKS"])
            self.st("fin", self.nvs_d[:, 127, :], VKS[:, 0:256], reads=["VKS"])
            self.st("fin", self.nks_d[:, 127, :], VKS[:, 256:512], reads=["VKS"])
            self.st("x1w0", self.vs_d[:, :], VKS[:, 0:256], reads=["VKS"], writes=["vs_d"], final=False)

    def attn_block(self, qc0, nq, qpos, prev, cur):
        P = self.P
        KT, VT, M = self.KT, self.VT, self.MREG
        ql = qc0 - MC0
        qsg = segs(qc0, qc0 + nq)
        EB = self.SCR[:, 2048:3072].bitcast(BF16).rearrange("p (r x) -> p r x", r=4)
        DEN = self.SCR[:, 3072:4096].rearrange("p (r x) -> p r x", r=2)
        for c in range(2):
            bn = P.bank()
            bd = P.bank()
            groups = []
            gres = []
            for par in range(2):
                kv = 2 * c + par
                r0 = par * 64
                rows = slice(r0, r0 + 64)
                es = []
                for kb, (kc0, vblk, mask) in enumerate((prev, cur)):
                    ei = par * 2 + kb
                    b = P.bank()
                    ksg = segs(kc0, kc0 + 128)
                    S = self.PS[b][:, 0:4 * nq].rearrange("p (g q) -> p g q", g=4)
                    self.mm([(S, KT[rows, c, kc0:kc0 + 128], M[rows, 4 * c:4 * c + 4, ql:ql + nq]),
                             (S, self.IDB[:, :], mask[:, :, qpos:qpos + nq])],
                            [("k", c, s) for s in ksg] + [("M", 4 * c + g, s) for g in range(4) for s in qsg] + ["IDB", "masks"],
                            [("ps", b)])
                    E = EB[:, ei, 0:4 * nq].rearrange("p (g q) -> p g q", g=4)
                    self.act(E, S, AF.Exp, [("ps", b)], [("eb", ei)])
                    es.append((ei, E, vblk))
                NUM = self.PS[bn][rows, 0:4 * nq].rearrange("p (g q) -> p g q", g=4)
                DN = self.PS[bd][rows, 0:4 * nq].rearrange("p (g q) -> p g q", g=4)
                groups.append([(NUM, VT[:, vblk, kv * 64:(kv + 1) * 64], E) for (ei, E, vblk) in es])
                groups.append([(DN, self.ONESB[:, 0:64], E) for (ei, E, vblk) in es])
                gres += [("eb", ei) for (ei, _, _) in es] + [("v", vblk) for (_, _, vblk) in es]
            self.mm_multi(groups, gres + ["ONESB"], [("ps", bn), ("ps", bd)])
            dslot = c
            NUMA = self.PS[bn][:, 0:4 * nq].rearrange("p (g q) -> p g q", g=4)
            DNA = self.PS[bd][:, 0:4 * nq].rearrange("p (g q) -> p g q", g=4)
            DS = DEN[:, dslot, 0:4 * nq].rearrange("p (g q) -> p g q", g=4)
            sk = self.SINKP[:, c, :].unsqueeze(2).broadcast_to([128, 4, nq])
            self.tt("dve", DS, DNA, sk, ALU.add, [("ps", bd), "SINKP"], [("den", dslot)])
            self.act(DS, DS, AF.Ln, [("den", dslot)], [("den", dslot)])
            self.act(DS, DS, AF.Exp, [("den", dslot)], [("den", dslot)], scale=-1.0)
            self.tt("dve", M[:, 8 + 4 * c:8 + 4 * c + 4, ql:ql + nq], NUMA, DS, ALU.mult,
                    [("ps", bn), ("den", dslot)], [("M", 8 + 4 * c + g, s) for g in range(4) for s in qsg])

    def p4_attn(self, h, ada=False):
        P = self.P
        KT, VT, M = self.KT, self.VT, self.MREG
        EB = self.SCR[:, 2048:3072].bitcast(BF16).rearrange("p (r x) -> p r x", r=4)
        DEN = self.SCR[:, 3072:4096].rearrange("p (r x) -> p r x", r=2)
        blocks = []
        if h == 0:
            o = C_SB - MC0
            P.op("dve", lambda e: e.memset(self.MREG[:, 8:16, o:o + 50], 0.0), [],
                 [("M", 8 + q, s_) for q in range(8) for s_ in (1, 2)])
            blocks.append((C_SB, 2, 126, (0, 0, self.MPREV), (128, 1, self.MCUR)))
        for lb in range(8):
            qc0 = C_OWN + 128 * lb
            if lb == 0:
                prev = (128, 1, self.MPREV0 if h == 0 else self.MPREV)
            else:
                prev = (qc0 - 128, 1 + lb, self.MPREV)
            blocks.append((qc0, 128, 0, prev, (qc0, 2 + lb, self.MCUR)))
        steps = [(blk, kv) for blk in blocks for kv in range(4)]
        T = len(steps)
        st = {}

        def stage_A(t):
            (qc0, nq, qpos, prev, cur), kv = steps[t]
            c, par = kv // 2, kv % 2
            rows = slice(par * 64, par * 64 + 64)
            ql = qc0 - MC0
            qsg = segs(qc0, qc0 + nq)
            es = []
            for kb, (kc0, vblk, mask) in enumerate((prev, cur)):
                b = (2 * t + kb) % 4
                ei = (2 * t + kb) % 4
                ksg = segs(kc0, kc0 + 128)
                S = self.PS[b][:, 0:4 * nq].rearrange("p (g q) -> p g q", g=4)
                self.mm([(S, KT[rows, c, kc0:kc0 + 128], M[rows, 4 * c:4 * c + 4, ql:ql + nq]),
                         (S, self.IDB[:, :], mask[:, :, qpos:qpos + nq])],
                        [("k", c, s) for s in ksg] + [("M", 4 * c + g, s) for g in range(4) for s in qsg] + ["IDB", "masks"],
                        [("ps", b)])
                E = EB[:, ei, 0:4 * nq].rearrange("p (g q) -> p g q", g=4)
                self.act(E, S, AF.Exp, [("ps", b)], [("eb", ei)])
                es.append((ei, E, vblk))
            st[t] = es

        def stage_C(t):
            (qc0, nq, qpos, prev, cur), kv = steps[t]
            c, par = kv // 2, kv % 2
            rows = slice(par * 64, par * 64 + 64)
            u = t // 2
            bn = 4 + 2 * (u % 2)
            bd = bn + 1
            es = st.pop(t)
            NUM = self.PS[bn][rows, 0:4 * nq].rearrange("p (g q) -> p g q", g=4)
            DN = self.PS[bd][rows, 0:4 * nq].rearrange("p (g q) -> p g q", g=4)
            self.mm_multi([[(NUM, VT[:, vblk, kv * 64:(kv + 1) * 64], E) for (ei, E, vblk) in es],
                           [(DN, self.ONESB[:, 0:64], E) for (ei, E, vblk) in es]],
                          [("eb", ei) for (ei, _, _) in es] + [("v", vblk) for (_, _, vblk) in es] + ["ONESB"],
                          [("psr", bn, par), ("psr", bd, par)])

        def stage_D(u):
            (qc0, nq, qpos, prev, cur), kv = steps[2 * u]
            c = kv // 2
            bn = 4 + 2 * (u % 2)
            bd = bn + 1
            DNA = self.PS[bd][:, 0:4 * nq].rearrange("p (g q) -> p g q", g=4)
            DS = DEN[:, u % 2, 0:4 * nq].rearrange("p (g q) -> p g q", g=4)
            sk = self.SINKP[:, c, :].unsqueeze(2).broadcast_to([128, 4, nq])
            self.tt("dve", DS, DNA, sk, ALU.add, [("psr", bd, 0), ("psr", bd, 1), ("ps", bd), "SINKP"], [("den", u % 2)])

        def stage_EF(u):
            (qc0, nq, qpos, prev, cur), kv = steps[2 * u]
            c = kv // 2
            ql = qc0 - MC0
            qsg = segs(qc0, qc0 + nq)
            bn = 4 + 2 * (u % 2)
            NUMA = self.PS[bn][:, 0:4 * nq].rearrange("p (g q) -> p g q", g=4)
            DS = DEN[:, u % 2, 0:4 * nq].rearrange("p (g q) -> p g q", g=4)
            self.act(DS, DS, AF.Ln, [("den", u % 2)], [("den", u % 2)])
            self.act(DS, DS, AF.Exp, [("den", u % 2)], [("den", u % 2)], scale=-1.0)
            self.tt("dve", M[:, 8 + 4 * c:8 + 4 * c + 4, ql:ql + nq], NUMA, DS, ALU.mult,
                    [("psr", bn, 0), ("psr", bn, 1), ("ps", bn), ("den", u % 2)],
                    [("M", 8 + 4 * c + g, s) for g in range(4) for s in qsg])

        for t in range(T + 3):
            if ada and 2 <= t < 34:
                self.p1_ada(1, ms=[16 + t - 2], finish=False, bank=(2 * t + 2) % 4)
            if t == (T - 14 if ada else 2):
                self.P.dma("pool", "wo", lambda e: e.dma_start(out=self.WO, in_=self.wo_d.rearrange("(k p) n -> p k n", p=128)),
                           reads=["WDdone"], writes=["WO"])
            if h == 0 and t == (T - 10 if ada else 4):
                self.p4s_loads()
            if ada and t == T - 6:
                self.cache_copies()
            if t < T:
                stage_A(t)
            if 1 <= t <= T:
                stage_C(t - 1)
                if (t - 1) % 2 == 1:
                    stage_D((t - 1) // 2)
            if t >= 3 and (t - 3) % 2 == 0 and (t - 3) // 2 < T // 2:
                pass
            if t >= 3 and (t - 3) % 2 == 0:
                u = (t - 3) // 2
                if u < T // 2:
                    stage_EF(u)
        if ada:
            assert T >= 34
            self.psr_sync()
            self.p1_ada(1, ms=[], finish=True)

    def p4s_loads(self):
        P = self.P
        KC = self.SCR[:, 4096:6144].bitcast(BF16).rearrange("p (b f) -> p b f", b=16)
        P.dma("pool", "p4s", lambda e: e.dma_start(out=KC, in_=self.ck_d.rearrange("b s f -> s b f")), writes=["KC"])
        P.dma("pool", "p4s", lambda e: e.dma_start(out=self.VC, in_=self.cv_d.rearrange("b s f -> s b f")),
              writes=["VC"])
        VFL = self.SCR[0:1, 0:2048].bitcast(BF16)
        P.dma("pool", "p4s", lambda e: e.dma_start(out=VFL, in_=self.vs_d.rearrange("b f -> (b f)").unsqueeze(0)),
              reads=["vs_d"], writes=["VFL", "WK2"])
        P.end_group("p4s")

    def p4_samples(self):
        P = self.P
        KT, M = self.KT, self.MREG
        KC = self.SCR[:, 4096:6144].bitcast(BF16).rearrange("p (b f) -> p b f", b=16)
        VFL = self.SCR[0:1, 0:2048].bitcast(BF16)
        for c in range(2):
            for bh in range(2):
                b = P.bank()
                pv = self.psb(b)
                self.tr([(pv[:, i * 128:(i + 1) * 128], KC[:, bh * 8 + i, c * 128:(c + 1) * 128], self.IDB[:, :])
                         for i in range(8)], ["KC", "IDB"], [("ps", b)])
                self.copy("act", self.KCT[:, c, bh * 8:bh * 8 + 8, :].rearrange("p b s -> p (b s)"), pv[:, :],
                          [("ps", b)], ["KCT"])
        bS = P.bank()
        bN = P.bank()
        qs = C_SMP - MC0
        groups = []
        for b_ in range(16):
            for kv in range(4):
                c, par = kv // 2, kv % 2
                rows = slice(par * 64, par * 64 + 64)
                rhs = M[rows, 4 * c:4 * c + 4, qs + b_:qs + b_ + 1]
                o = (b_ * 4 + kv) * 4
                groups.append([(self.PS[bS][:, o:o + 4].unsqueeze(2), self.KCT[rows, c, b_, :], rhs)])
                groups.append([(self.PS[bN][0:1, o:o + 4].unsqueeze(2), KT[rows, c, C_SMP + b_:C_SMP + b_ + 1], rhs)])
        self.mm_multi(groups, ["KCT"] + [("k", c, 2) for c in range(2)] + [("M", q, 2) for q in range(8)],
                      [("ps", bS), ("ps", bN)])
        ES = self.SCR[:, 2048:2176].bitcast(BF16)
        EN = self.SCR[0:1, 2176:2304].bitcast(BF16)
        self.act(ES, self.PS[bS][:, 0:256], AF.Exp, [("ps", bS)], ["ES"])
        self.act(EN, self.PS[bN][0:1, 0:256], AF.Exp, [("ps", bN)], ["EN"])
        bO = P.bank()
        bD = P.bank()
        groups = []
        for b_ in range(16):
            for kv in range(4):
                c, par = kv // 2, kv % 2
                r0 = par * 64
                o = (b_ * 4 + kv) * 4
                oo = (b_ * 2 + c) * 4
                out = self.PS[bO][r0:r0 + 64, oo:oo + 4]
                groups.append([(out, self.VC[:, b_, kv * 64:(kv + 1) * 64], ES[:, o:o + 4]),
                               (out, VFL[0:1, b_ * 256 + kv * 64:b_ * 256 + kv * 64 + 64], EN[0:1, o:o + 4])])
        groups.append([(self.PS[bD][:, 0:256], self.ONESB[:, :], ES[:, :]),
                       (self.PS[bD][:, 0:256], self.ONESB[0:1, :], EN[0:1, :])])
        self.mm_multi(groups, ["VC", "VFL", "ES", "EN", "ONESB"], [("ps", bO), ("ps", bD)])
        DENS = self.SCR[:, 2304:2560]
        self.tt("dve", DENS.rearrange("p (b k) -> p b k", b=16), self.PS[bD][:, 0:256].rearrange("p (b k) -> p b k", b=16),
                self.SINKE[:, :].unsqueeze(1).broadcast_to([128, 16, 16]), ALU.add, [("ps", bD), "SINKE"], ["DENS"])
        self.P.op("dve", lambda e: e.reciprocal(out=DENS, in_=DENS), ["DENS"], ["DENS"])
        for par in range(2):
            r0 = par * 64
            rows = slice(r0, r0 + 64)
            num = self.PS[bO][rows, 0:128].rearrange("p (b c g) -> p c g b", b=16, c=2)
            den = DENS[rows, :].rearrange("p (b c q g) -> p c q g b", b=16, c=2, q=2)[:, :, par]
            out = M[rows, 8:16, qs:qs + 16].rearrange("p (c g) b -> p c g b", c=2)
            self.tt("dve", out, num, den, ALU.mult, [("ps", bO), "DENS"], [("M", 8 + q, 2) for q in range(8)])

    def p5_conv(self, h):
        P = self.P
        HT, M = self.HT, self.MREG
        tiles = self.ntiles(h, 252)
        XIN = self.SCR[:, 0:1024].rearrange("p (r x) -> p r x", r=2)
        U = self.SCR[:, 1024:2052].rearrange("p (r x) -> p r x", r=2)
        TC = self.SCR[:, 2052:3076].rearrange("p (r x) -> p r x", r=2)
        cw = lambda i, j: self.PF[:, PF_CW + i * 8 + j:PF_CW + i * 8 + j + 1]
        it = 0
        first_d = True
        for j in range(8):
            wx, wxr = self.w_use(self.wp(h, 10 + 3 * j))
            wc, wcr = self.w_use(self.wp(h, 10 + 3 * j + 1))
            wb, wbr = self.w_use(self.wp(h, 10 + 3 * j + 2))
            for ti, (c0, n) in enumerate(tiles):
                small = (h == 0 and ti == 0)
                big_i = ti - (1 if h == 0 else 0)
                sg = segs(c0, c0 + n)
                hres = [("h", k, s) for k in range(8) for s in sg]
                sl = it % 2
                it += 1
                b1, b2, b3 = P.bank(), P.bank(), P.bank()
                self.mm([(self.PS[b1][:, 0:n], wx[:, k, :], HT[:, k, c0:c0 + n]) for k in range(8)], [wxr] + hres, [("ps", b1)])
                self.mm([(self.PS[b2][:, 0:n], wc[:, k, :], HT[:, k, c0:c0 + n]) for k in range(8)], [wcr] + hres, [("ps", b2)])
                if small:
                    self.mm([(self.PS[b3][:, 0:48], wb[:, k, :], HT[:, k, C_SB:C_SB + 48]) for k in range(8)], [wbr] + hres, [("ps", b3)])
                    self.copy("act", XIN[:, sl, 0:n], self.PS[b1][:, 0:n], [("ps", b1)], [("xin", sl)])
                    self.tt("dve", self.USM[:, j, :], self.PS[b2][:, 0:n], XIN[:, sl, 0:n], ALU.mult,
                            [("ps", b2), ("xin", sl)], [("usm", j)])
                    self.copy("act", self.BSM[:, j, :], self.PS[b3][:, 0:48], [("ps", b3)], [("bsm", j)])
                    continue
                self.mm([(self.PS[b3][:, 0:n], wb[:, k, :], HT[:, k, c0:c0 + n]) for k in range(8)], [wbr] + hres, [("ps", b3)])
                self.copy("act", XIN[:, sl, 0:n], self.PS[b1][:, 0:n], [("ps", b1)], [("xin", sl)])
                if big_i == 0:
                    if h == 0:
                        self.ts("dve", U[:, sl, 0:2], self.USM[:, j, 2:4], self.PF[:, PF_FL:PF_FL + 1], None, ALU.mult, None,
                                [("usm", j), "PF"], [("u", sl)])
                    else:
                        self.copy("dve", U[:, sl, 0:2], self.ULAST[:, j, :], [("ulast", j)], [("u", sl)])
                else:
                    self.copy("dve", U[:, sl, 0:2], U[:, 1 - sl, 512:514], [("u", 1 - sl)], [("u", sl)])
                self.tt("dve", U[:, sl, 2:2 + n], self.PS[b2][:, 0:n], XIN[:, sl, 0:n], ALU.mult,
                        [("ps", b2), ("xin", sl)], [("u", sl)])
                if big_i == 1:
                    self.copy("dve", self.ULAST[:, j, :], U[:, sl, 512:514], [("u", sl)], [("ulast", j)])
                self.ts("dve", TC[:, sl, 0:n], U[:, sl, 0:n], cw(0, j), None, ALU.mult, None, [("u", sl), "PF"], [("tc", sl)])
                self.stt("dve", TC[:, sl, 0:n], U[:, sl, 1:1 + n], cw(1, j), TC[:, sl, 0:n], ALU.mult, ALU.add,
                         [("u", sl), ("tc", sl), "PF"], [("tc", sl)])
                self.stt("dve", TC[:, sl, 0:n], U[:, sl, 2:2 + n], cw(2, j), TC[:, sl, 0:n], ALU.mult, ALU.add,
                         [("u", sl), ("tc", sl), "PF"], [("tc", sl)])
                wres = [("M", 16 + j, s) for s in sg]
                if h == 0 and first_d:
                    wres = wres + ["KCT", "VC"]
                    first_d = False
                self.tt("dve", M[:, 16 + j, c0 - MC0:c0 - MC0 + n], self.PS[b3][:, 0:n], TC[:, sl, 0:n], ALU.mult,
                        [("ps", b3), ("tc", sl)], wres)
        if h == 0:
            self.p5_small()

    def p5_small(self):
        P = self.P
        M = self.MREG
        USM, BSM, CSM, STA = self.USM, self.BSM, self.CSM, self.STA
        ures = [("usm", j) for j in range(8)]
        ST = self.SCR[0:32, 3076:4100]
        self.ld("st", ST, self.sca_d.rearrange("b t f -> (b t) f"), writes=["ST"])
        b = P.bank()
        self.tr([(self.PS[b][:, j * 32:(j + 1) * 32], ST[:, j * 128:(j + 1) * 128], self.CST[0:32, 0:32]) for j in range(8)],
                ["ST", "CST"], [("ps", b)])
        self.copy("dve", STA[:, :, :].rearrange("p j x -> p (j x)"), self.PS[b][:, 0:256], [("ps", b)], ["STA"])
        wbc = lambda i, n: self.PF[:, PF_CW + i * 8:PF_CW + i * 8 + 8].unsqueeze(2).broadcast_to([128, 8, n])
        c2 = CSM[:, :, 0:2]
        self.tt("dve", c2, USM[:, :, 0:2], wbc(0, 2), ALU.mult, ures + ["PF"], ["csm"])
        t2 = self.SCR[:, 4100:4116].rearrange("p (j x) -> p j x", j=8)
        self.tt("dve", t2, USM[:, :, 1:3], wbc(1, 2), ALU.mult, ures + ["PF"], ["t2"])
        self.tt("dve", c2, c2, t2, ALU.add, ["csm", "t2"], ["csm"])
        self.tt("dve", t2, USM[:, :, 2:4], wbc(2, 2), ALU.mult, ures + ["PF", "csm"], ["t2"])
        self.tt("dve", c2, c2, t2, ALU.add, ["csm", "t2"], ["csm"])
        st3 = STA[:, :, :].rearrange("p j (b t) -> p j b t", t=2)
        cs = CSM[:, :, 32:48]
        t16 = self.SCR[:, 4116:4244].rearrange("p (j x) -> p j x", j=8)
        self.tt("dve", cs, st3[:, :, :, 0], wbc(0, 16), ALU.mult, ["STA", "PF", "csm"], ["csm"])
        self.tt("dve", t16, st3[:, :, :, 1], wbc(1, 16), ALU.mult, ["STA", "PF"], ["t16"])
        self.tt("dve", cs, cs, t16, ALU.add, ["csm", "t16"], ["csm"])
        self.tt("dve", t16, USM[:, :, 34:50], wbc(2, 16), ALU.mult, ures + ["PF", "csm"], ["t16"])
        self.tt("dve", cs, cs, t16, ALU.add, ["csm", "t16"], ["csm"])
        o = C_SB - MC0
        P.op("dve", lambda e: e.memset(M[:, 16:24, o:o + 48], 0.0), [], [("M", 16 + j, s) for j in range(8) for s in (1, 2)])
        self.tt("dve", M[:, 16:24, o:o + 2], BSM[:, :, 0:2], c2, ALU.mult, [("bsm", j) for j in range(8)] + ["csm"],
                [("M", 16 + j, 1) for j in range(8)])
        self.tt("dve", M[:, 16:24, o + 32:o + 48], BSM[:, :, 32:48], cs, ALU.mult, [("bsm", j) for j in range(8)] + ["csm"],
                [("M", 16 + j, 2) for j in range(8)])
        b = P.bank()
        self.tr([(self.PS[b][0:16, j * 128:(j + 1) * 128], USM[:, j, 34:50], self.CST[:, 0:128]) for j in range(4)],
                ures + ["CST"], [("ps", b)])
        b2 = P.bank()
        self.tr([(self.PS[b2][0:16, j * 128:(j + 1) * 128], USM[:, 4 + j, 34:50], self.CST[:, 0:128]) for j in range(4)],
                ures + ["CST"], [("ps", b2)])
        UST = self.SCR[0:16, 4244:5268]
        self.copy("dve", UST[:, 0:512], self.PS[b][0:16, :], [("ps", b)], ["UST"])
        self.copy("dve", UST[:, 512:1024], self.PS[b2][0:16, :], [("ps", b2)], ["UST"])
        self.st("fin", self.ncas_d[:, 1, :], UST, reads=["UST"])

    def p6_gates(self, h):
        P = self.P
        HT, M = self.HT, self.MREG
        tiles = self.ntiles(h, C_SB)
        SG = self.SCR[:, 0:2048].rearrange("p (r x) -> p r x", r=4)
        T1 = self.SCR[:, 2048:4096].rearrange("p (r x) -> p r x", r=4)
        it = 0
        for j in range(8):
            wa, war = self.w_use(self.wp(h, 34 + 4 * j))
            wb, wbr = self.w_use(self.wp(h, 34 + 4 * j + 1))
            wga, wgar = self.w_use(self.wp(h, 34 + 4 * j + 2))
            wgb, wgbr = self.w_use(self.wp(h, 34 + 4 * j + 3))
            for (c0, n) in tiles:
                sg = segs(c0, c0 + n)
                l0 = c0 - MC0
                sl = it % 2
                it += 1
                ba, bb, bga, bgb = P.bank(), P.bank(), P.bank(), P.bank()
                hres = [("h", k, s) for k in range(8) for s in sg]
                self.mm([(self.PS[bga][:, 0:n], wga[:, k, :], HT[:, k, c0:c0 + n]) for k in range(8)], [wgar] + hres, [("ps", bga)])
                self.mm([(self.PS[bgb][:, 0:n], wgb[:, k, :], HT[:, k, c0:c0 + n]) for k in range(8)], [wgbr] + hres, [("ps", bgb)])
                self.mm([(self.PS[ba][:, 0:n], wa[:, k, :], M[:, 16 + k, l0:l0 + n]) for k in range(8)],
                        [war] + [("M", 16 + k, s) for k in range(8) for s in sg], [("ps", ba)])
                self.mm([(self.PS[bb][:, 0:n], wb[:, k, :], M[:, 8 + k, l0:l0 + n]) for k in range(8)],
                        [wbr] + [("M", 8 + k, s) for k in range(8) for s in sg], [("ps", bb)])
                self.act(SG[:, sl * 2, 0:n], self.PS[bga][:, 0:n], AF.Sigmoid, [("ps", bga)], [("sg", sl * 2)])
                self.act(SG[:, sl * 2 + 1, 0:n], self.PS[bgb][:, 0:n], AF.Sigmoid, [("ps", bgb)], [("sg", sl * 2 + 1)])
                self.tt("dve", T1[:, sl * 2, 0:n], self.PS[ba][:, 0:n], SG[:, sl * 2, 0:n], ALU.mult,
                        [("ps", ba), ("sg", sl * 2)], [("t1", sl * 2)])
                self.tt("dve", T1[:, sl * 2 + 1, 0:n], self.PS[bb][:, 0:n], SG[:, sl * 2 + 1, 0:n], ALU.mult,
                        [("ps", bb), ("sg", sl * 2 + 1)], [("t1", sl * 2 + 1)])
                self.tt("dve", M[:, j, l0:l0 + n], T1[:, sl * 2, 0:n], T1[:, sl * 2 + 1, 0:n], ALU.add,
                        [("t1", sl * 2), ("t1", sl * 2 + 1)], [("M", j, s) for s in sg])

    def resid_block(self, n, c0, nk, W, wres, mbase, xsrc, gt_tile, gt_res, xslot, x1res, korder=None, pre=None):
        P = self.P
        M = self.MREG
        l0 = c0 - MC0
        sg = segs(c0, c0 + n)
        ns = getattr(self, "x1_slots", 2)
        X1 = self.SCR[:, 0:1024 * ns].rearrange("p (r x) -> p r x", r=ns)
        TT = self.SCR[:, 1024 * ns:1024 * ns + 1024].rearrange("p (r x) -> p r x", r=2)
        for half in range(2):
            if pre is not None:
                b = pre[half]
                self.mm([(self.PS[b][0:n, :], M[:, mbase + k, l0:l0 + n], W[:, k, half * 512:(half + 1) * 512]) for k in korder],
                        wres + [("M", mbase + k, s) for k in korder for s in sg], [("ps", b)], start=False, stop=True)
            else:
                b = P.bank()
                self.mm([(self.PS[b][0:n, :], M[:, mbase + k, l0:l0 + n], W[:, k, half * 512:(half + 1) * 512])
                         for k in (korder if korder is not None else range(nk))],
                        wres + [("M", mbase + k, s) for k in range(nk) for s in sg], [("ps", b)])
            self.tt("dve", TT[0:n, half, :], self.PS[b][0:n, :], gt_tile[0:n, half * 512:(half + 1) * 512], ALU.mult,
                    [("ps", b)] + gt_res, [("tt", half)])
            self.tt("dve", X1[0:n, xslot, half * 512:(half + 1) * 512], TT[0:n, half, :], xsrc[:, half * 512:(half + 1) * 512],
                    ALU.add, [("tt", half)] + x1res, [("x1", xslot)])
        return X1[0:n, xslot, :]

    def p7_wo(self, h):
        P = self.P
        X1S = self.GT1SB
        self.x1_slots = 3
        blocks = []
        if h == 0:
            blocks.append(("small", C_SB, 48, C_SB))
        blocks += [("own", C_OWN + 128 * lb, 128, (C_OWN if h == 0 else NCOL) + 128 * lb) for lb in range(8)]
        qa = []
        q2 = None
        for it, (kind, c0, n, row0) in enumerate(blocks):
            xs = it % 2
            x3 = it % 3
            if q2 is not None:
                self.norm2(*q2)
                q2 = None
            if len(qa) == 2:
                a = qa.pop(0)
                self.norm1b(*a[0])
                q2 = a[1]
            if it == 0:
                self.ld("xr%d" % xs, self.XR[0:n, xs, :], self.x_d[row0:row0 + n, :], writes=[("xr", xs)])
            if it + 1 < len(blocks):
                (_, _, n2, row2) = blocks[it + 1]
                xs2 = (it + 1) % 2
                self.ld("xr%d" % xs2, self.XR[0:n2, xs2, :], self.x_d[row2:row2 + n2, :], writes=[("xr", xs2)])
            gt = self.GT1SB if kind == "small" else self.GT1BC
            gres = ["GT1SB"] if kind == "small" else ["GT1BC"]
            x1 = self.resid_block(n, c0, 8, self.WO, ["WO"], 0, self.XR[0:n, xs, :], gt, gres, x3, [("xr", xs)])
            ssc = self.ss_col
            self.ss_col = (self.ss_col + 1) % 64
            if kind == "small":
                self.copy("act", X1S[:, :], x1, [("x1", x3)], ["GT1SB"])
            else:
                lb = (c0 - C_OWN) // 128
                r = h * 1024 + lb * 128
                self.st("x1w%d" % x3, self.x1_d[r:r + 128, :], x1, reads=[("x1", x3)], writes=[("x1d", h, lb)], final=False)
            self.norm1a(n, x1, [("x1", x3)], ssc)
            qa.append(((n, x1, [("x1", x3)], ssc, xs), (n, xs, c0, 24, 32, (kind == "small"), 2)))
        if q2 is not None:
            self.norm2(*q2)
        for a in qa:
            self.norm1b(*a[0])
            self.norm2(*a[1])
        self.x1_slots = 2

    def p8_ffn(self, h):
        P = self.P
        HT, M = self.HT, self.MREG
        tiles = self.ntiles(h, C_SB)
        UP = self.SCR[:, 0:2056].rearrange("p (r x) -> p r x", r=4)
        TG = self.SCR[:, 2056:4104].rearrange("p (r x) -> p r x", r=4)
        SGL = self.SCR[:, 4104:5128].rearrange("p (r x) -> p r x", r=2)
        UPSM = self.SCR[:, 5128:5128 + 792].rearrange("p (c x) -> p c x", c=44)
        self.UPSM = UPSM
        wdv = self.wd_d.rearrange("(k p) n -> p k n", p=128)
        eres_hi = [("k", c, s_) for c in range(2) for s_ in range(11)] + [("v", i) for i in range(10)] + ["WO", "KCARdummy"]
        for i, (k0, k1) in enumerate(((11, 17), (17, 22))):
            self.P.dma("pool", "wd", lambda e, k0=k0, k1=k1: e.dma_start(out=self.WD[:, k0:k1, :], in_=wdv[:, k0:k1, :]),
                       writes=[("WD", 2 + i)] + (eres_hi if i == 0 else []))
        self.P.end_group("wd")
        fw = lambda i, c: self.PF[:, PF_FW + i * 44 + c:PF_FW + i * 44 + c + 1]
        fb = lambda c: self.PF[:, PF_FB + c:PF_FB + c + 1]
        it = 0
        tail = None
        for p in range(22):
            chunks = (p, 22 + p)
            ws = [self.w_use(self.wp(h, 66 + 2 * p)), self.w_use(self.wp(h, 66 + 2 * p + 1))]
            for ti, (c0, n) in enumerate(tiles):
                small = (h == 0 and ti == 0)
                big_i = ti - (1 if h == 0 else 0)
                sg = segs(c0, c0 + n)
                hres = [("h", k, s) for k in range(8) for s in sg]
                sl = it % 2
                if not small:
                    it += 1
                tres = []
                for hi, ch in enumerate(chunks):
                    w, wr = ws[hi]
                    b = P.bank()
                    us = sl * 2 + hi
                    self.mm([(self.PS[b][:, 0:n], w[:, k, :], HT[:, k, c0:c0 + n]) for k in range(8)], [wr] + hres, [("ps", b)])
                    if small:
                        self.copy("act", UPSM[:, ch, 0:2], self.PS[b][:, 0:2], [("ps", b)], [("upsm", ch)])
                        self.copy("act", UPSM[:, ch, 2:18], self.PS[b][:, 32:48], [("ps", b)], [("upsm", ch)])
                        continue
                    self.copy("act", UP[:, us, 2:2 + n], self.PS[b][:, 0:n], [("ps", b)], [("up", us)])
                    self.act(TG[:, us, 0:n], self.PS[b][:, 0:n], AF.Identity, [("ps", b), "PF"], [("tg", us)],
                             bias=fb(ch), scale=fw(2, ch))
                    if big_i == 0:
                        if h == 0:
                            self.ts("pool", UP[:, us, 0:2], UPSM[:, ch, 0:2], self.PF[:, PF_FL:PF_FL + 1], None, ALU.mult, None,
                                    [("upsm", ch), "PF"], [("up", us)])
                        else:
                            self.copy("pool", UP[:, us, 0:2], self.UPLAST[:, ch, :], [("uplast", ch)], [("up", us)])
                    else:
                        self.copy("pool", UP[:, us, 0:2], UP[:, (1 - sl) * 2 + hi, 512:514], [("up", (1 - sl) * 2 + hi)], [("up", us)])
                    if big_i == 1:
                        self.copy("pool", self.UPLAST[:, ch, :], UP[:, us, 512:514], [("up", us)], [("uplast", ch)])
                    self.stt("dve", TG[:, us, 0:n], UP[:, us, 1:1 + n], fw(1, ch), TG[:, us, 0:n], ALU.mult, ALU.add,
                             [("up", us), ("tg", us), "PF"], [("tg", us)])
                    self.stt("dve", TG[:, us, 0:n], UP[:, us, 0:n], fw(0, ch), TG[:, us, 0:n], ALU.mult, ALU.add,
                             [("up", us), ("tg", us), "PF"], [("tg", us)])
                    tres.append(("tg", us))
                if small:
                    continue
                if tail is not None:
                    self.p8_tail(*tail)
                tail = (SGL, TG, M, sl, n, p, c0, sg)
        if tail is not None:
            self.p8_tail(*tail)
        self.scr_switch()
        self.wd_low_loads()
        if h == 0:
            self.p8_samples()

    def p8_tail(self, SGL, TG, M, sl, n, p, c0, sg):
        self.act(SGL[:, sl, 0:n], TG[:, sl * 2, 0:n], AF.Silu, [("tg", sl * 2)], [("sgl", sl)])
        self.tt("pool", M[:, p, c0 - MC0:c0 - MC0 + n], SGL[:, sl, 0:n], TG[:, sl * 2 + 1, 0:n], ALU.mult,
                [("sgl", sl), ("tg", sl * 2 + 1)], [("M", p, s) for s in sg])

    def p8_samples(self):
        P = self.P
        M = self.MREG
        UPSM = self.UPSM
        ures = [("upsm", c) for c in range(44)]
        STF = self.SCR[:, 0:1408].rearrange("p (c x) -> p c x", c=44)
        STT = self.SCR[0:32, 1408:2816]
        sffv = self.sff_d.rearrange("b t f -> (b t) f")
        for q in range(4):
            self.ld("st", STT, sffv[:, q * 1408:(q + 1) * 1408], writes=["STT"])
            for g0 in range(0, 11, 8):
                ng = min(8, 11 - g0)
                b = P.bank()
                self.tr([(self.PS[b][:, i * 32:(i + 1) * 32], STT[:, (g0 + i) * 128:(g0 + i + 1) * 128], self.CST[0:32, 0:32])
                         for i in range(ng)], ["STT", "CST"], [("ps", b)])
                self.copy("dve", STF[:, q * 11 + g0:q * 11 + g0 + ng, :].rearrange("p c x -> p (c x)"),
                          self.PS[b][:, 0:ng * 32], [("ps", b)], ["STF"])
        st4 = STF[:, :, :].rearrange("p c (b t) -> p c b t", t=2)
        CV = self.SCR[:, 2816:3520].rearrange("p (c x) -> p c x", c=44)
        T = self.SCR[:, 3520:4224].rearrange("p (c x) -> p c x", c=44)
        wbc = lambda i: self.PF[:, PF_FW + i * 44:PF_FW + i * 44 + 44].unsqueeze(2).broadcast_to([128, 44, 16])
        bbc = self.PF[:, PF_FB:PF_FB + 44].unsqueeze(2).broadcast_to([128, 44, 16])
        self.tt("dve", CV, st4[:, :, :, 0], wbc(0), ALU.mult, ["STF", "PF"], ["CV"])
        self.tt("dve", T, st4[:, :, :, 1], wbc(1), ALU.mult, ["STF", "PF"], ["T44"])
        self.tt("dve", CV, CV, T, ALU.add, ["CV", "T44"], ["CV"])
        self.tt("dve", T, UPSM[:, :, 2:18], wbc(2), ALU.mult, ures + ["PF", "CV"], ["T44"])
        self.tt("dve", CV, CV, T, ALU.add, ["CV", "T44"], ["CV"])
        self.tt("dve", CV, CV, bbc, ALU.add, ["CV", "PF"], ["CV"])
        self.act(T[:, 0:22, :], CV[:, 0:22, :], AF.Silu, ["CV", "T44"], ["T44"])
        o = C_SMP - MC0
        self.tt("dve", M[:, 0:22, o:o + 16], T[:, 0:22, :], CV[:, 22:44, :], ALU.mult, ["T44", "CV"],
                [("M", c, 2) for c in range(22)])
        UT = self.SCR[0:16, 1408:2816]
        for q in range(4):
            for g0 in range(0, 11, 4):
                ng = min(4, 11 - g0)
                b = P.bank()
                self.tr([(self.PS[b][0:16, i * 128:(i + 1) * 128], UPSM[:, q * 11 + g0 + i, 2:18], self.CST[:, 0:128])
                         for i in range(ng)], ures + ["CST"], [("ps", b)])
                self.copy("dve", UT[:, g0 * 128:(g0 + ng) * 128], self.PS[b][0:16, 0:ng * 128], [("ps", b)], ["STT"])
            self.st("st", self.nfs_d[:, 1, q * 1408:(q + 1) * 1408], UT, reads=["STT"])

    def wd_low_loads(self):
        eres = [("h", k, s) for k in range(8) for s in range(11)]
        wdv = self.wd_d.rearrange("(k p) n -> p k n", p=128)
        for i, (k0, k1) in enumerate(((0, 6), (6, 11))):
            self.P.dma("pool", "wdl", lambda e, k0=k0, k1=k1: e.dma_start(out=self.WD[:, k0:k1, :], in_=wdv[:, k0:k1, :]),
                       writes=[("WD", i)] + (eres if i == 0 else []))
        self.P.end_group("wdl")

    def p9_down(self, h):
        P = self.P
        M = self.MREG
        X1S = self.GT1SB
        pend = None
        blocks = [("own", C_OWN + 128 * lb, 128) for lb in range(8)]
        if h == 0:
            blocks.append(("small", C_SB, 48))
        khigh = list(range(11, 22))
        klow = list(range(0, 11))
        pre = {}
        for bi in range(4):
            (kind, c0, n) = blocks[bi]
            l0 = c0 - MC0
            sg = segs(c0, c0 + n)
            bb = []
            for half in range(2):
                b = P.bank()
                self.mm([(self.PS[b][0:n, :], M[:, k, l0:l0 + n], self.WD[:, k, half * 512:(half + 1) * 512]) for k in khigh],
                        [("WD", 2), ("WD", 3)] + [("M", k, s_) for k in khigh for s_ in sg], [("ps", b)], start=True, stop=False)
                bb.append(b)
            pre[bi] = bb

        def load(i):
            (kind, c0, n) = blocks[i]
            if kind == "small":
                return
            xs = i % 2
            lb = (c0 - C_OWN) // 128
            r = h * 1024 + lb * 128
            self.ld("xr%d" % xs, self.XR[0:n, xs, :], self.x1_d[r:r + 128, :], reads=[("x1d", h, lb)], writes=[("xr", xs)])

        load(0)
        for i, (kind, c0, n) in enumerate(blocks):
            xs = i % 2
            if i + 1 < len(blocks):
                load(i + 1)
            if kind == "small":
                xsrc, xres = X1S[:, :], ["GT1SB"]
                gt, gres = self.GT2SB, ["GT2SB"]
            else:
                xsrc, xres = self.XR[0:n, xs, :], [("xr", xs)]
                gt, gres = self.GT2BC, ["GT2BC"]
            if i in pre:
                x2 = self.resid_block(n, c0, 22, self.WD, [("WD", 0), ("WD", 1)], 0, xsrc, gt, gres, xs, xres,
                                       korder=klow, pre=pre[i])
            else:
                x2 = self.resid_block(n, c0, 22, self.WD, [("WD", j) for j in range(4)], 0, xsrc, gt, gres, xs, xres,
                                       korder=khigh + klow)
            if pend is not None:
                self.p9_epilogue(*pend)
            pend = (h, kind, c0, n, x2, xs)
        self.p9_epilogue(*pend)
        bm = P.bank()
        self.P.op("pe", lambda e: e.matmul(self.PS[bm][0:1, 0:1], lhsT=self.ONESB[0:1, 0:1], rhs=self.ONESB[0:1, 0:1],
                                           start=True, stop=True), ["ONESB"], [("ps", bm), "WDdone"])

    def p9_epilogue(self, h, kind, c0, n, x2, xs):
        YB = self.SCR[:, 3072:5120].rearrange("p (r x) -> p r x", r=2)[:, xs, :]
        ssc = self.ss_col
        self.ss_col = (self.ss_col + 1) % 64
        self.act(self.JK[0:n, :], x2, AF.Square, [("x1", xs)], ["JK", ("ss", ssc)], accum_out=self.SS[0:n, ssc:ssc + 1])
        self.rstd(n, ssc)
        self.stt("dve", YB[0:n, :], x2, self.RS[0:n, ssc:ssc + 1], self.GFIN[0:n, :], ALU.mult, ALU.mult,
                 [("x1", xs), ("rs", ssc), "GFIN"], [("YB", xs)])
        if kind == "small":
            self.st("yw%d" % xs, self.ys_d[:, :], YB[32:48, :], reads=[("YB", xs)])
        else:
            lb = (c0 - C_OWN) // 128
            r = h * 1024 + lb * 128
            self.st("yw%d" % xs, self.y_d[r:r + 128, :], YB[0:128, :], reads=[("YB", xs)])

    def final_outputs(self):
        P = self.P
        b = P.bank()
        b2 = P.bank()
        self.tr([(self.PS[b][0:2, j * 128:(j + 1) * 128], self.ULAST[:, j, :], self.CST[:, 0:128]) for j in range(4)],
                [("ulast", j) for j in range(8)] + ["CST"], [("ps", b)])
        self.tr([(self.PS[b2][0:2, j * 128:(j + 1) * 128], self.ULAST[:, 4 + j, :], self.CST[:, 0:128]) for j in range(4)],
                [("ulast", j) for j in range(8)] + ["CST"], [("ps", b2)])
        UO = self.SCR[0:2, 0:1024]
        self.copy("dve", UO[:, 0:512], self.PS[b][0:2, :], [("ps", b)], ["UO"])
        self.copy("dve", UO[:, 512:1024], self.PS[b2][0:2, :], [("ps", b2)], ["UO"])
        self.st("fin", self.ncap_d[:, :], UO, reads=["UO"])
        FO = self.SCR[0:2, 1024:1024 + 5632]
        for g0 in range(0, 44, 4):
            b = P.bank()
            self.tr([(self.PS[b][0:2, i * 128:(i + 1) * 128], self.UPLAST[:, g0 + i, :], self.CST[:, 0:128]) for i in range(4)],
                    [("uplast", c) for c in range(44)] + ["CST"], [("ps", b)])
            self.copy("dve", FO[:, g0 * 128:(g0 + 4) * 128], self.PS[b][0:2, :], [("ps", b)], ["FO"])
        self.st("fin", self.nfp_d[:, :], FO, reads=["FO"])

    def dtap(self, name, ap, shape):
        if not self.debug:
            return
        self.P.barrier()
        d = self.nc.dram_tensor("dbg_" + name, list(shape), ap.dtype, kind="ExternalOutput").ap()
        self.taps.append(name)
        self.st("dbg_" + name, d, ap)

    def psr_sync(self):
        for b in range(4, 8):
            self.P.op("pe", lambda e, b=b: e.matmul(self.PS[b][0:1, 0:1], lhsT=self.ONESB[0:1, 0:1], rhs=self.ONESB[0:1, 0:1],
                                                    start=True, stop=True),
                      ["ONESB"], [("ps", b), ("psr", b, 0), ("psr", b, 1)])

    def scr_switch(self):
        self.P.phase_switch(lambda e: e.memset(self.TOK[:], 0.0))

    def reg_scr(self):
        A = self.P.add_alias
        A("CT", 0, 384); A("tmpm", 1024, 1536); A("scr_smp", 6656, 6784)
        A("WK2", 0, 1024); A("VL", 1024, 1280); A("KL", 1280, 1536); A("VKS", 1536, 2048)
        for i in range(4):
            A(("eb", i), 2048 + i * 256, 2048 + (i + 1) * 256)
            A(("sg", i), i * 512, (i + 1) * 512)
            A(("t1", i), 2048 + i * 512, 2048 + (i + 1) * 512)
            A(("up", i), i * 514, (i + 1) * 514)
            A(("tg", i), 2056 + i * 512, 2056 + (i + 1) * 512)
        for u in range(2):
            A(("den", u), 3072 + u * 512, 3072 + (u + 1) * 512)
            A(("xin", u), u * 512, (u + 1) * 512)
            A(("u", u), 1024 + u * 514, 1024 + (u + 1) * 514)
            A(("tc", u), 2052 + u * 512, 2052 + (u + 1) * 512)
            A(("x1", u), u * 1024, (u + 1) * 1024)
            A(("x1", 2), 2048, 3072)
            A(("tt", u), 2048 + u * 512, 2048 + (u + 1) * 512)
            A(("YB", u), 3072 + u * 1024, 3072 + (u + 1) * 1024)
            A(("sgl", u), 4104 + u * 512, 4104 + (u + 1) * 512)
        A("KC", 4096, 6144); A("VFL", 0, 2048); A("ES", 2048, 2176); A("EN", 2176, 2304); A("DENS", 2304, 2560)
        A("ST", 3076, 4100); A("t2", 4100, 4116); A("t16", 4116, 4244); A("UST", 4244, 5268)
        for j in range(8):
            A(("usm", j), 5268 + j * 50, 5268 + (j + 1) * 50)
            A(("bsm", j), 5668 + j * 48, 5668 + (j + 1) * 48)
        A("csm", 6052, 6436); A("STA", 6436, 6692)
        for ch in range(44):
            A(("upsm", ch), 5128 + ch * 18, 5128 + (ch + 1) * 18)
        A("STF", 0, 1408); A("STT", 1408, 2816); A("CV", 2816, 3520); A("T44", 3520, 4224)
        A("UO", 0, 1024); A("FO", 1024, 6656)

    def build(self):
        import os
        self.reg_scr()
        stop = int(os.environ.get("K_STOP", "99"))
        P = self.P
        self.w_init()
        if self.debug:
            P.op("dve", lambda e: e.memset(self.EREG[:, :], 0.0), [], ["dbgz"])
            P.op("dve", lambda e: e.memset(self.MREGF[:, :], 0.0), [], ["dbgz"])
            P.barrier()
        self.p0_consts()
        if stop >= 1:
            self.p1_ada(0)
        stop0 = stop
        stop1 = int(os.environ.get("K_STOP1", "99"))
        for h in self.halves:
            stop = stop0 if h == 0 else stop1
            if stop < 2:
                break
            self.scr_switch()
            self.p2_h1(h)
            self.dtap("H1_%d" % h, self.EREG[:, 0:8 * NCOL], [128, 8 * NCOL])
            if stop < 3:
                break
            self.scr_switch()
            self.p3_qkv(h)
            self.dtap("KT_%d" % h, self.EREG[:, 8 * NCOL:10 * NCOL], [128, 2 * NCOL])
            self.dtap("VT_%d" % h, self.EREG[:, 10 * NCOL:10 * NCOL + 2560], [128, 2560])
            self.dtap("Q_%d" % h, self.MREGF[:, 0:8 * MW], [128, 8 * MW])
            if stop < 4:
                break
            self.scr_switch()
            self.psr_sync()
            self.p4_attn(h, ada=(h == self.halves[0]))
            self.psr_sync()
            self.dtap("ADAT", self.ADAT[:].rearrange("p a b -> p (a b)"), [128, 48 * 48])
            if h == 0 and stop != 4:
                self.scr_switch()
                self.p4_samples()
            self.dtap("ATT_%d" % h, self.MREGF[:, 8 * MW:16 * MW], [128, 8 * MW])
            if stop < 5:
                break
            self.scr_switch()
            self.p5_conv(h)
            self.dtap("BC_%d" % h, self.MREGF[:, 16 * MW:24 * MW], [128, 8 * MW])
            if stop < 6:
                break
            self.scr_switch()
            self.p6_gates(h)
            self.dtap("G_%d" % h, self.MREGF[:, 0:8 * MW], [128, 8 * MW])
            if stop < 7:
                break
            self.scr_switch()
            self.p7_wo(h)
            self.dtap("H2_%d" % h, self.EREG[:, 0:8 * NCOL], [128, 8 * NCOL])
            if stop < 8:
                break
            self.scr_switch()
            self.p8_ffn(h)
            self.dtap("MT_%d" % h, self.MREGF[:, 0:22 * MW], [128, 22 * MW])
            if stop < 9:
                break
            self.scr_switch()
            self.p9_down(h)
        if stop0 >= 10 and stop1 >= 10:
            self.scr_switch()
            self.final_outputs()
        self.stats = P.finalize_and_emit()
        return self.nc


def _chunks(W):
    K, N = W.shape
    return np.ascontiguousarray(W.reshape(K // 128, 128, N // 128, 128).transpose(2, 1, 0, 3))


def _qperm():
    idx = []
    for c in range(2):
        for g in range(4):
            for hh in (8 * c + g, 8 * c + 4 + g):
                idx.extend(range(hh * 64, hh * 64 + 64))
    return np.array(idx)


def _host_prep(inp):
    f = lambda a: np.ascontiguousarray(np.asarray(a, dtype=np.float32))
    w_in = f(inp["w_in"])[0]
    xin_w, b_w, c_w = w_in[:, 0:1024], w_in[:, 1024:2048], w_in[:, 2048:3072]
    q_w, k_w, v_w = w_in[:, 3072:4096], w_in[:, 4096:4352], w_in[:, 4352:4608]
    ga_w, gb_w = w_in[:, 4608:5632], w_in[:, 5632:6656]
    perm = _qperm()
    wa = f(inp["w_a_out"])[0]
    wb = f(inp["w_b_out"])[0][perm, :]
    w_up = f(inp["w_up"])[0]
    cx, cc, cb = _chunks(xin_w), _chunks(c_w), _chunks(b_w)
    ca, cbm, cga, cgb = _chunks(wa), _chunks(wb), _chunks(ga_w), _chunks(gb_w)
    cu = _chunks(w_up)
    parts = [_chunks(f(inp["w_ada"])[0]), _chunks(k_w), _chunks(q_w[:, perm])]
    for j in range(8):
        parts += [cx[j:j + 1], cc[j:j + 1], cb[j:j + 1]]
    for j in range(8):
        parts += [ca[j:j + 1], cbm[j:j + 1], cga[j:j + 1], cgb[j:j + 1]]
    for p in range(22):
        parts += [cu[p:p + 1], cu[22 + p:23 + p]]
    parts.append(np.zeros((2, 128, 8, 128), np.float32))
    WS = np.ascontiguousarray(np.concatenate(parts, 0))
    assert WS.shape[0] == 160

    def fm(v, n):
        return np.asarray(v, np.float32).reshape(n, 128).T

    pf = np.zeros((128, NPF), np.float32)
    pf[:, PF_GM:PF_GM + 8] = fm(inp["g_mix"][0], 8)
    caw = f(inp["conv_a_w"])[0]
    for i in range(3):
        pf[:, PF_CW + i * 8:PF_CW + i * 8 + 8] = fm(caw[i], 8)
    pf[:, PF_GF:PF_GF + 8] = fm(inp["g_ffn"][0], 8)
    fcw = f(inp["ffn_conv_w"])[0]
    for i in range(3):
        pf[:, PF_FW + i * 44:PF_FW + i * 44 + 44] = fm(fcw[i], 44)
    pf[:, PF_FB:PF_FB + 44] = fm(inp["ffn_conv_b"][0], 44)
    pf[:, PF_BA:PF_BA + 48] = fm(inp["b_ada"][0], 48)
    pr = np.zeros((1, 1040), np.float32)
    pr[0, 0:1024] = inp["g_final"]
    pr[0, 1024:1040] = inp["attn_sinks"][0]
    cst = np.zeros((128, 384), np.float32)
    cst[:, 0:128] = np.eye(128, dtype=np.float32)
    s = np.arange(128)[:, None]
    q = np.arange(128)[None, :]
    cst[:, 128:256] = (s >= q)
    cst[:, 256:384] = (s <= q)
    xp = f(inp["x_prompt"])[0]
    xs = f(inp["x_sample"])[:, 0]
    cp = f(inp["c_prompt"])[0]
    cs = f(inp["c_sample"])
    shared = dict(WS=WS, wv=f(v_w), wk=f(k_w), wo=f(inp["w_o"])[0], wd=f(inp["w_down"])[0], pr=pr, cst=cst)
    maps = []
    for i in range(NCORES):
        X = np.zeros((2352, D), np.float32)
        if i > 0:
            X[0:256] = xp[2048 * i - 256:2048 * i]
        X[C_SMP:C_SMP + 16] = xs[16 * i:16 * i + 16]
        X[C_OWN:C_OWN + 2048] = xp[2048 * i:2048 * i + 2048]
        ct = np.zeros((48, D), np.float32)
        ct[0] = cp
        ct[1] = cp
        ct[32:48] = cs[16 * i:16 * i + 16]
        cT = np.ascontiguousarray(ct.reshape(48, 8, 128).transpose(2, 1, 0)).reshape(128, 8 * 48)
        pfi = pf.copy()
        pfi[:, PF_FL] = 0.0 if i == 0 else 1.0
        m = dict(shared)
        m.update(x=X, cT=cT, pf=pfi,
                 ck=f(inp["cache_k_win"])[0, 16 * i:16 * i + 16].reshape(16, 128, 256),
                 cv=f(inp["cache_v_win"])[0, 16 * i:16 * i + 16].reshape(16, 128, 256),
                 sca=f(inp["state_conv_a"])[0, 16 * i:16 * i + 16],
                 sff=f(inp["state_ffn_conv"])[0, 16 * i:16 * i + 16])
        maps.append(m)
    return maps


def _assemble(results):
    r = results
    y_prompt = np.concatenate([r[i]["y"] for i in range(NCORES)], 0)[None]
    y_sample = np.concatenate([r[i]["ys"] for i in range(NCORES)], 0)[:, None, :]
    nca_p = r[NCORES - 1]["ncap"][None, None]
    nca_s = np.concatenate([r[i]["ncas"] for i in range(NCORES)], 0)[None]
    nk_p = r[NCORES - 1]["nkp"].reshape(1, 1, 128, 4, 64)
    nk_s = np.concatenate([r[i]["nks"] for i in range(NCORES)], 0).reshape(1, 128, 128, 4, 64)
    nv_p = r[NCORES - 1]["nvp"].reshape(1, 1, 128, 4, 64)
    nv_s = np.concatenate([r[i]["nvs"] for i in range(NCORES)], 0).reshape(1, 128, 128, 4, 64)
    nf_p = r[NCORES - 1]["nfp"][None, None]
    nf_s = np.concatenate([r[i]["nfs"] for i in range(NCORES)], 0)[None]
    outs = (y_prompt, y_sample, nca_p, nca_s, nk_p, nk_s, nv_p, nv_s, nf_p, nf_s)
    return tuple(np.ascontiguousarray(o, dtype=np.float32) for o in outs)


def kernel(**inputs):
    maps = _host_prep(inputs)
    bld = Builder()
    nc = bld.build()
    res = run_bass_kernel_spmd(nc, maps, core_ids=list(range(NCORES)))
    return _assemble(res.results)
```
